# Optimizing a Trainium2 kernel written in Bass

```python
import jax
import jax.numpy as jnp
from jax import lax
import numpy as np

D_MODEL = 1024
BATCH = 2
SEQ = 8192
DEPTH = 2
DEC_BATCH = 8
DEC_SEQ = 8192
PAST_LEN = 128

GRID_W = 64
HEAD_DIM = 64
NA_HEADS = 6
NA_WIN_H = 8
NA_WIN_W = 16
NA_DIM = NA_HEADS * HEAD_DIM
SC_DIM = 256
SC_WIDTH = 3
SWA_HEADS = 6
SWA_KV_HEADS = 2
SWA_GROUP = SWA_HEADS // SWA_KV_HEADS
SWA_WINDOW = 128
SWA_BLOCK = 128
SWA_DIM = SWA_HEADS * HEAD_DIM
SWA_KV_DIM = SWA_KV_HEADS * HEAD_DIM
T5_BUCKETS = 32
T5_MAX_DIST = 128
MIX_DIM = NA_DIM + SC_DIM + SWA_DIM
IN_SPLITS = (NA_DIM, NA_DIM, NA_DIM, SC_DIM, SC_DIM, SC_DIM, SWA_DIM, SWA_KV_DIM, SWA_KV_DIM)
IN_DIM = 3 * NA_DIM + 3 * SC_DIM + SWA_DIM + 2 * SWA_KV_DIM
MEM_LEN = 256
XA_HEADS = 4
XA_HEAD_DIM = 128
XA_DIM = XA_HEADS * XA_HEAD_DIM
PEER_HEADS = 8
PEER_NKEYS = 128
PEER_EXPERTS = PEER_NKEYS * PEER_NKEYS
PEER_TOPK = 16
PEER_KEY_DIM = 256
PEER_HALF = PEER_KEY_DIM // 2
PEER_CHUNK = 128
RMS_EPS = 1e-6
NEG_INF = -1e30

kernel_name = "hybrid_na2d_shortconv_swa_peer_encoder"


def rmsnorm(x, g):
    xf = x.astype(jnp.float32)
    y = xf * lax.rsqrt(jnp.mean(xf * xf, axis=-1, keepdims=True) + RMS_EPS)
    return (y * g.astype(jnp.float32)).astype(x.dtype)


def t5_bucket(rel):
    nb = T5_BUCKETS // 2
    max_exact = nb // 2
    ret = (rel > 0).astype(np.int32) * nb
    n = np.abs(rel)
    large = max_exact + (np.log(np.maximum(n, 1) / max_exact) / np.log(T5_MAX_DIST / max_exact) * (nb - max_exact)).astype(np.int32)
    large = np.minimum(large, nb - 1)
    return (ret + np.where(n < max_exact, n, large)).astype(np.int32)


def neighborhood_attention(q, k, v, rpb):
    b, s = q.shape[0], q.shape[1]
    rows = s // GRID_W
    kh = min(NA_WIN_H, rows)
    grid = lambda t: t.reshape(b, rows, GRID_W, NA_HEADS, HEAD_DIM)
    qg, kg, vg = grid(q), grid(k), grid(v)
    col = np.arange(GRID_W)
    col_start = np.clip(col - NA_WIN_W // 2, 0, GRID_W - NA_WIN_W)
    col_idx = col_start[:, None] + np.arange(NA_WIN_W)[None, :]
    dc_idx = col_idx - col[:, None] + (NA_WIN_W - 1)
    scale = HEAD_DIM ** -0.5

    def row_block(r):
        r0 = jnp.clip(r - kh // 2, 0, rows - kh)
        rows_idx = r0 + jnp.arange(kh)
        k_r = jnp.take(kg, rows_idx, axis=1)[:, :, col_idx]
        v_r = jnp.take(vg, rows_idx, axis=1)[:, :, col_idx]
        q_r = lax.dynamic_index_in_dim(qg, r, axis=1, keepdims=False)
        logits = jnp.einsum("bchd,bicjhd->bhcij", q_r, k_r).astype(jnp.float32) * scale
        dr_idx = rows_idx - r + (NA_WIN_H - 1)
        bias = rpb[dr_idx[None, :, None], dc_idx[:, None, :]]
        logits = logits + jnp.transpose(bias, (3, 0, 1, 2)).astype(jnp.float32)
        p = jax.nn.softmax(logits.reshape(b, NA_HEADS, GRID_W, kh * NA_WIN_W), axis=-1).reshape(logits.shape)
        return jnp.einsum("bhcij,bicjhd->bchd", p.astype(v.dtype), v_r)

    out = lax.map(row_block, jnp.arange(rows))
    return jnp.transpose(out, (1, 0, 2, 3, 4)).reshape(b, s, NA_DIM)


def short_conv_mixer(gate_b, gate_c, hx, w):
    u = gate_c * hx
    half = SC_WIDTH // 2
    s = u.shape[1]
    up = jnp.pad(u, ((0, 0), (half, half), (0, 0)))
    y = up[:, 0:s] * w[0]
    for tap in range(1, SC_WIDTH):
        y = y + up[:, tap:tap + s] * w[tap]
    return gate_b * y


def window_gqa(q, k, v, bias_off, sink):
    b, s = q.shape[0], q.shape[1]
    blk = SWA_BLOCK
    nb = s // blk
    qb = q.reshape(b, nb, blk, SWA_KV_HEADS, SWA_GROUP, HEAD_DIM)
    pad = ((0, 0), (blk, blk), (0, 0), (0, 0))
    kp = jnp.pad(k, pad).reshape(b, nb + 2, blk, SWA_KV_HEADS, HEAD_DIM)
    vp = jnp.pad(v, pad).reshape(b, nb + 2, blk, SWA_KV_HEADS, HEAD_DIM)
    kband = jnp.concatenate([kp[:, :-2], kp[:, 1:-1], kp[:, 2:]], axis=2)
    vband = jnp.concatenate([vp[:, :-2], vp[:, 1:-1], vp[:, 2:]], axis=2)
    logits = jnp.einsum("bnqhgd,bnjhd->bnhgqj", qb, kband).astype(jnp.float32) * (HEAD_DIM ** -0.5)
    a = np.arange(blk)[:, None]
    j = np.arange(3 * blk)[None, :]
    off = j - blk - a
    in_win = np.abs(off) <= SWA_WINDOW
    key_pos = (np.arange(nb)[:, None] - 1) * blk + np.arange(3 * blk)[None, :]
    valid = (key_pos >= 0) & (key_pos < s)
    mask = in_win[None, :, :] & valid[:, None, :]
    bias = bias_off[np.clip(off + SWA_WINDOW, 0, 2 * SWA_WINDOW)]
    bias = jnp.transpose(bias.reshape(blk, 3 * blk, SWA_KV_HEADS, SWA_GROUP), (2, 3, 0, 1)).astype(jnp.float32)
    logits = jnp.where(mask[None, :, None, None], logits + bias[None, None], NEG_INF)
    sk = sink.astype(jnp.float32).reshape(1, 1, SWA_KV_HEADS, SWA_GROUP, 1, 1)
    m = jnp.maximum(jnp.max(logits, axis=-1, keepdims=True), sk)
    p = jnp.exp(logits - m)
    p = p / (jnp.sum(p, axis=-1, keepdims=True) + jnp.exp(sk - m))
    out = jnp.einsum("bnhgqj,bnjhd->bnqhgd", p.astype(v.dtype), vband)
    return out.reshape(b, s, SWA_DIM)


def mem_cross_attention(xn, memn, wq, wk, wv, wo):
    b, s = xn.shape[0], xn.shape[1]
    m = memn.shape[1]
    q = (xn @ wq).reshape(b, s, XA_HEADS, XA_HEAD_DIM)
    k = (memn @ wk).reshape(b, m, XA_HEADS, XA_HEAD_DIM)
    v = (memn @ wv).reshape(b, m, XA_HEADS, XA_HEAD_DIM)
    logits = jnp.einsum("bshd,bmhd->bhsm", q, k).astype(jnp.float32) * (XA_HEAD_DIM ** -0.5)
    p = jax.nn.softmax(logits, axis=-1)
    o = jnp.einsum("bhsm,bmhd->bshd", p.astype(v.dtype), v).reshape(b, s, XA_DIM)
    return o @ wo


def peer_ffn(xn, wq, subkeys, u, v):
    b, s, d = xn.shape
    xt = xn.reshape(-1, PEER_CHUNK, d)

    def chunk(xc):
        q = (xc @ wq).reshape(PEER_CHUNK, PEER_HEADS, 2, PEER_HALF)
        scores = jnp.einsum("thpe,hpne->thpn", q, subkeys)
        top_s, top_i = lax.top_k(scores, PEER_TOPK)
        cand = top_s[:, :, 0, :, None] + top_s[:, :, 1, None, :]
        cand_idx = top_i[:, :, 0, :, None] * PEER_NKEYS + top_i[:, :, 1, None, :]
        cand = cand.reshape(PEER_CHUNK, PEER_HEADS, PEER_TOPK * PEER_TOPK)
        cand_idx = cand_idx.reshape(PEER_CHUNK, PEER_HEADS, PEER_TOPK * PEER_TOPK)
        best_s, best_pos = lax.top_k(cand, PEER_TOPK)
        expert = jnp.take_along_axis(cand_idx, best_pos, axis=-1)
        gate = jax.nn.softmax(best_s.astype(jnp.float32), axis=-1)
        act = jax.nn.gelu(jnp.einsum("thkd,td->thk", u[expert], xc).astype(jnp.float32), approximate=False)
        w = (gate * act).astype(xc.dtype)
        return jnp.einsum("thk,thkd->td", w, v[expert])

    return lax.map(chunk, xt).reshape(b, s, d)


def encoder_trunk(x, mem, norm_mix_g, w_in, na_rpb, conv_w, swa_sink, t5_bias, w_out,
                  norm_xa_g, norm_mem_g, w_xq, w_xk, w_xv, w_xo,
                  norm_ffn_g, peer_wq, peer_subkeys, peer_u, peer_v, final_g):
    b, s = x.shape[0], x.shape[1]
    offsets = np.arange(-SWA_WINDOW, SWA_WINDOW + 1)
    bias_off = t5_bias[t5_bucket(offsets)]
    split_at = [int(i) for i in np.cumsum(IN_SPLITS)[:-1]]
    for l in range(DEPTH):
        h = rmsnorm(x, norm_mix_g[l])
        z = h @ w_in[l]
        na_q, na_k, na_v, sc_b, sc_c, sc_h, sw_q, sw_k, sw_v = jnp.split(z, split_at, axis=-1)
        y_na = neighborhood_attention(na_q.reshape(b, s, NA_HEADS, HEAD_DIM),
                                      na_k.reshape(b, s, NA_HEADS, HEAD_DIM),
                                      na_v.reshape(b, s, NA_HEADS, HEAD_DIM), na_rpb[l])
        y_sc = short_conv_mixer(sc_b, sc_c, sc_h, conv_w[l])
        y_sw = window_gqa(sw_q.reshape(b, s, SWA_HEADS, HEAD_DIM),
                          sw_k.reshape(b, s, SWA_KV_HEADS, HEAD_DIM),
                          sw_v.reshape(b, s, SWA_KV_HEADS, HEAD_DIM), bias_off, swa_sink[l])
        x = x + jnp.concatenate([y_na, y_sc, y_sw], axis=-1) @ w_out[l]
        x = x + mem_cross_attention(rmsnorm(x, norm_xa_g[l]), rmsnorm(mem, norm_mem_g[l]),
                                    w_xq[l], w_xk[l], w_xv[l], w_xo[l])
        x = x + peer_ffn(rmsnorm(x, norm_ffn_g[l]), peer_wq[l], peer_subkeys[l], peer_u[l], peer_v[l])
    return rmsnorm(x, final_g)


def setup_inputs(seed: int = 0) -> dict:
    key = jax.random.key(seed)
    ks = jax.random.split(key, 24)
    f32 = jnp.float32
    nrm = lambda k, shape, scale: jax.random.normal(k, shape, f32) * scale
    gain = lambda k, shape: 1.0 + 0.02 * jax.random.normal(k, shape, f32)
    return {
        "x_prompt": nrm(ks[0], (BATCH, SEQ, D_MODEL), 1.0),
        "x_sample": nrm(ks[1], (DEC_BATCH, DEC_SEQ, D_MODEL), 1.0),
        "mem_prompt": nrm(ks[2], (BATCH, MEM_LEN, D_MODEL), 1.0),
        "mem_sample": nrm(ks[3], (DEC_BATCH, MEM_LEN, D_MODEL), 1.0),
        "norm_mix_g": gain(ks[4], (DEPTH, D_MODEL)),
        "w_in": nrm(ks[5], (DEPTH, D_MODEL, IN_DIM), D_MODEL ** -0.5),
        "na_rpb": nrm(ks[6], (DEPTH, 2 * NA_WIN_H - 1, 2 * NA_WIN_W - 1, NA_HEADS), 0.1),
        "conv_w": nrm(ks[7], (DEPTH, SC_WIDTH, SC_DIM), SC_WIDTH ** -0.5),
        "swa_sink": nrm(ks[8], (DEPTH, SWA_HEADS), 0.5),
        "t5_bias": nrm(ks[9], (T5_BUCKETS, SWA_HEADS), 0.1),
        "w_out": nrm(ks[10], (DEPTH, MIX_DIM, D_MODEL), MIX_DIM ** -0.5),
        "norm_xa_g": gain(ks[11], (DEPTH, D_MODEL)),
        "norm_mem_g": gain(ks[12], (DEPTH, D_MODEL)),
        "w_xq": nrm(ks[13], (DEPTH, D_MODEL, XA_DIM), D_MODEL ** -0.5),
        "w_xk": nrm(ks[14], (DEPTH, D_MODEL, XA_DIM), D_MODEL ** -0.5),
        "w_xv": nrm(ks[15], (DEPTH, D_MODEL, XA_DIM), D_MODEL ** -0.5),
        "w_xo": nrm(ks[16], (DEPTH, XA_DIM, D_MODEL), XA_DIM ** -0.5),
        "norm_ffn_g": gain(ks[17], (DEPTH, D_MODEL)),
        "peer_wq": nrm(ks[18], (DEPTH, D_MODEL, PEER_HEADS * PEER_KEY_DIM), D_MODEL ** -0.5),
        "peer_subkeys": nrm(ks[19], (DEPTH, PEER_HEADS, 2, PEER_NKEYS, PEER_HALF), PEER_HALF ** -0.5),
        "peer_u": nrm(ks[20], (DEPTH, PEER_EXPERTS, D_MODEL), D_MODEL ** -0.5),
        "peer_v": nrm(ks[21], (DEPTH, PEER_EXPERTS, D_MODEL), D_MODEL ** -0.5),
        "final_g": gain(ks[22], (D_MODEL,)),
    }


def reference(x_prompt, x_sample, mem_prompt, mem_sample, norm_mix_g, w_in, na_rpb, conv_w,
              swa_sink, t5_bias, w_out, norm_xa_g, norm_mem_g, w_xq, w_xk, w_xv, w_xo,
              norm_ffn_g, peer_wq, peer_subkeys, peer_u, peer_v, final_g):
    y_prompt = encoder_trunk(x_prompt, mem_prompt, norm_mix_g, w_in, na_rpb, conv_w, swa_sink, t5_bias, w_out,
                             norm_xa_g, norm_mem_g, w_xq, w_xk, w_xv, w_xo,
                             norm_ffn_g, peer_wq, peer_subkeys, peer_u, peer_v, final_g)
    y_sample = encoder_trunk(x_sample, mem_sample, norm_mix_g, w_in, na_rpb, conv_w, swa_sink, t5_bias, w_out,
                             norm_xa_g, norm_mem_g, w_xq, w_xk, w_xv, w_xo,
                             norm_ffn_g, peer_wq, peer_subkeys, peer_u, peer_v, final_g)
    return (y_prompt, y_sample)
```

```python
import os
import numpy as np
import concourse.bass as bass
import concourse.mybir as mybir
from concourse.bass_utils import run_bass_kernel_spmd
from contextlib import ExitStack

F32 = mybir.dt.float32
BF16 = mybir.dt.bfloat16
U32 = mybir.dt.uint32
ALU = mybir.AluOpType
AF = mybir.ActivationFunctionType
AX = mybir.AxisListType

D = 1024
DEPTH = 2
NEG = -1e30
SB_BASE = 16512
SB_END = 229312


class Buf:
    __slots__ = ("name", "lw", "rd")

    def __init__(self, name):
        self.name = name
        self.lw = None
        self.rd = {}


class Sched:
    CE = ("pe", "act", "dve", "pool")
    ENG = ("pe", "act", "dve", "pool", "sp")
    EPOCH = 30000
    NDQ = 12

    def __init__(self, nc, es):
        self.nc = nc
        self.es = es
        self.sems = []
        self.prog = {e: [] for e in self.ENG}
        self.seen_c = {e: {} for e in self.ENG}
        self.seen_d = {e: {} for e in self.ENG}
        self.dq = {}
        for q in ("sp", "act", "pool"):
            self.dq[q] = dict(sems=[self._newsem("d_%s%d" % (q, i)) for i in range(self.NDQ)],
                              uses=[0] * self.NDQ, nxt=0)
        self.nops = 0
        self._cap = None

    def begin_capture(self):
        self._cap = []

    def end_capture(self):
        c = self._cap
        self._cap = None
        return c

    def feed(self, items):
        for it in items:
            if it[0] == "op":
                self.op(it[1], it[2], it[3], it[4])
            else:
                self.dma(it[1], it[2], it[3], it[4], it[5], it[6])

    def feed_merged(self, A, B):
        na, nb = len(A), len(B)
        if nb == 0:
            return self.feed(A)
        if na == 0:
            return self.feed(B)
        ia = ib = 0
        while ia < na or ib < nb:
            if ib >= nb or (ia < na and ia * nb <= ib * na):
                self.feed([A[ia]])
                ia += 1
            else:
                self.feed([B[ib]])
                ib += 1

    def _newsem(self, name):
        s = self.es.enter_context(self.nc.semaphore("%s_%d" % (name, len(self.sems))))
        self.sems.append(s)
        return len(self.sems) - 1

    def _deps(self, eng, reads, writes):
        need_c = {}
        need_d = {}

        def add(tok):
            if tok is None:
                return
            if tok[0] == "c":
                if need_c.get(tok[1], -1) < tok[2]:
                    need_c[tok[1]] = tok[2]
            else:
                if need_d.get(tok[1], 0) < tok[2]:
                    need_d[tok[1]] = tok[2]
        for b in reads:
            add(b.lw)
        for b in writes:
            add(b.lw)
            for t in b.rd.values():
                add(t)
        out = []
        sc, sd = self.seen_c[eng], self.seen_d[eng]
        for e2, i2 in need_c.items():
            if eng == "pe" and e2 == "pe":
                continue
            if sc.get(e2, -1) >= i2:
                continue
            sc[e2] = i2
            out.append(("c", e2, i2))
        for s, v in need_d.items():
            if sd.get(s, 0) >= v:
                continue
            sd[s] = v
            out.append(("d", s, v))
        return out

    def _update(self, tok, reads, writes):
        key = tok[1]
        for b in reads:
            old = b.rd.get(key)
            if old is None or old[2] < tok[2]:
                b.rd[key] = tok
        for b in writes:
            b.lw = tok
            b.rd = {}

    def op(self, eng, fn, reads=(), writes=(), inc=True):
        if self._cap is not None:
            self._cap.append(("op", eng, fn, tuple(reads), tuple(writes)))
            return
        waits = self._deps(eng, reads, writes)
        tok = ("c", eng, len(self.prog[eng]))
        self.prog[eng].append([waits, fn, "c", None])
        self._update(tok, reads, writes)
        self.nops += 1

    def dma(self, q, out, in_, reads=(), writes=(), slow=False):
        if self._cap is not None:
            self._cap.append(("dma", q, out, in_, tuple(reads), tuple(writes), slow))
            return
        ring = self.dq[q]
        k = ring["nxt"]
        ring["nxt"] = (k + 1) % self.NDQ
        waits = self._deps(q, reads, writes)
        s = ring["sems"][k]
        if ring["uses"][k] >= self.EPOCH // 16:
            s = ring["sems"][k] = self._newsem("d_" + q)
            ring["uses"][k] = 0
        if ring["uses"][k] > 0:
            v = 16 * ring["uses"][k]
            if self.seen_d[q].get(s, 0) < v:
                self.seen_d[q][s] = v
                waits.append(("d", s, v))
        ring["uses"][k] += 1
        tok = ("d", s, 16 * ring["uses"][k])
        if slow:
            fn = (lambda e, o=out, i=in_: e.dma_start(out=o, in_=i, allow_slow_non_contiguous=True))
        else:
            fn = (lambda e, o=out, i=in_: e.dma_start(out=o, in_=i))
        self.prog[q].append([waits, fn, "d", s])
        self._update(tok, reads, writes)
        self.nops += 1

    def wait_all(self, eng, bufs):
        waits = self._deps(eng, bufs, ())
        if waits:
            self.prog[eng].append([waits, None, "w", None])

    def barrier(self):
        toks = []
        for e in self.CE:
            last = None
            for i in range(len(self.prog[e]) - 1, -1, -1):
                if self.prog[e][i][2] == "c":
                    last = i
                    break
            if last is not None:
                toks.append(("c", e, last))
        for q in self.dq.values():
            for s, u in zip(q["sems"], q["uses"]):
                if u > 0:
                    toks.append(("d", s, 16 * u))
        for e in self.ENG:
            waits = []
            for t in toks:
                if t[0] == "c":
                    if self.seen_c[e].get(t[1], -1) >= t[2]:
                        continue
                    self.seen_c[e][t[1]] = t[2]
                else:
                    if self.seen_d[e].get(t[1], 0) >= t[2]:
                        continue
                    self.seen_d[e][t[1]] = t[2]
                waits.append(t)
            if waits:
                self.prog[e].append([waits, None, "w", None])

    def emit(self):
        nc = self.nc
        engobj = dict(pe="tensor", act="scalar", dve="vector", pool="gpsimd", sp="sync")
        need = {e: set() for e in self.ENG}
        for e in self.ENG:
            for waits, fn, kind, ds in self.prog[e]:
                for t in waits:
                    if t[0] == "c":
                        need[t[1]].add(t[2])
        sig = {}
        for e in self.CE:
            cur = self._newsem("c_" + e)
            cnt = 0
            for i in sorted(need[e]):
                if cnt >= self.EPOCH:
                    cur = self._newsem("c_" + e)
                    cnt = 0
                cnt += 1
                sig[(e, i)] = (cur, cnt)
        self.nsig = len(sig)
        sems = self.sems

        def replay(name):
            def f(e):
                for i, (waits, fn, kind, ds) in enumerate(self.prog[name]):
                    for t in waits:
                        if t[0] == "c":
                            s, v = sig[(t[1], t[2])]
                        else:
                            s, v = t[1], t[2]
                        e.wait_ge(sems[s], v)
                    if fn is None:
                        continue
                    ins = fn(e)
                    if kind == "d":
                        ins.then_inc(sems[ds], 16)
                    elif (name, i) in sig:
                        ins.then_inc(sems[sig[(name, i)][0]], 1)
            return f
        with nc.Block() as block:
            for name in self.ENG:
                getattr(block, engobj[name])(replay(name))


def t5_bucket(rel):
    nb = 16
    max_exact = 8
    ret = (rel > 0).astype(np.int32) * nb
    n = np.abs(rel)
    large = max_exact + (np.log(np.maximum(n, 1) / max_exact) / np.log(128 / max_exact) * (nb - max_exact)).astype(np.int32)
    large = np.minimum(large, nb - 1)
    return (ret + np.where(n < max_exact, n, large)).astype(np.int32)


def na_row0(r, rows):
    return int(np.clip(r - 4, 0, rows - 8))


def na_key_range(j, nt):
    rows = 2 * nt
    lo = na_row0(2 * j, rows)
    hi = na_row0(2 * j + 1, rows) + 7
    return lo // 2, hi // 2


def na_bias_block(rpb, nt, j, keys):
    nk = len(keys)
    out = np.zeros((128, 6, nk, 128), np.float32)
    if j < 0 or j >= nt:
        return out
    rows = 2 * nt
    qr = np.arange(128) // 64
    qc = np.arange(128) % 64
    r = 2 * j + qr
    r0 = np.clip(r - 4, 0, rows - 8)
    cs = np.clip(qc - 8, 0, 48)
    kr = np.arange(128) // 64
    kc = np.arange(128) % 64
    for ki, kt in enumerate(keys):
        if kt < 0 or kt >= nt:
            out[:, :, ki, :] = NEG
            continue
        R = 2 * kt + kr
        okr = (R[:, None] >= r0[None, :]) & (R[:, None] < r0[None, :] + 8)
        okc = (kc[:, None] >= cs[None, :]) & (kc[:, None] < cs[None, :] + 16)
        ok = okr & okc
        dr = np.clip(R[:, None] - r[None, :] + 7, 0, 14)
        dc = np.clip(kc[:, None] - qc[None, :] + 15, 0, 30)
        b = rpb[dr, dc]
        b = np.where(ok[:, :, None], b, np.float32(NEG))
        out[:, :, ki, :] = np.transpose(b, (0, 2, 1))
    return out


def swa_bias_block(bias_off, nt, j, keys):
    nk = len(keys)
    out = np.zeros((128, 6, nk, 128), np.float32)
    if j < 0 or j >= nt:
        return out
    a = np.arange(128)
    for ki, kt in enumerate(keys):
        if kt < 0 or kt >= nt:
            out[:, :, ki, :] = NEG
            continue
        off = (kt - j) * 128 + a[:, None] - a[None, :]
        ok = np.abs(off) <= 128
        b = bias_off[np.clip(off + 128, 0, 256)]
        b = np.where(ok[:, :, None], b, np.float32(NEG))
        out[:, :, ki, :] = np.transpose(b, (0, 2, 1))
    return out


class Cfg:
    def __init__(self, nt_f=64, nt_p=64, np_out=16, halo=3):
        self.NT_F = nt_f
        self.NT_P = nt_p
        self.NP_OUT = np_out
        self.H = halo
        self.S_P = np_out + 4 * halo if np_out > 0 else 0

    def f_special(self):
        nt = self.NT_F
        return [j for j in range(nt) if not (2 <= j <= nt - 3)]

    def f_swa_special(self):
        return [0, self.NT_F - 1]


class K:
    def __init__(self, cfg, dbg=False, stop=None, conv=True):
        self.cfg = cfg
        self.dbg = dbg
        self.stop = stop
        self.conv = conv
        self.nc = bass.Bass("TRN2", target_bir_lowering=False)
        self.uid = 0

    def sb(self, name, shape, dt):
        esz = 4 if dt in (F32, U32) else 2
        n = int(np.prod(shape[1:])) * esz
        n = (n + 31) // 32 * 32
        assert self.off + n <= SB_END, ("SBUF overflow", name, self.off, n)
        self.uid += 1
        t = self.nc.alloc_sbuf_tensor_at("%s_%d" % (name, self.uid), list(shape), dt, offset=self.off)
        self.off += n
        return t, Buf(name)

    def din(self, name, shape, dt=F32):
        return self.nc.dram_tensor(name, list(shape), dt, kind="ExternalInput").ap(), Buf(name)

    def dscr(self, name, shape, dt, out=False):
        kind = "ExternalOutput" if (out or self.dbg) else "Internal"
        return self.nc.dram_tensor(name, list(shape), dt, kind=kind).ap()

    def mm(self, out, lhsT, rhs, start, stop, r, w, inc=True):
        self.S.op("pe", lambda e: e.matmul(out, lhsT=lhsT, rhs=rhs, start=start, stop=stop), reads=r, writes=w, inc=inc)

    def tr(self, out, in_, r, w, inc=True):
        ident = self.ident
        self.S.op("pe", lambda e: e.transpose(out=out, in_=in_, identity=ident[:]), reads=list(r) + [self.b_ident], writes=w, inc=inc)

    def act(self, out, in_, func, r, w, **kw):
        self.S.op("act", lambda e: e.activation(out=out, in_=in_, func=func, **kw), reads=r, writes=w)

    def tt(self, eng, out, in0, in1, op, r, w):
        self.S.op(eng, lambda e: e.tensor_tensor(out=out, in0=in0, in1=in1, op=op), reads=r, writes=w)

    def ts(self, eng, out, in0, s1, s2, op0, op1, r, w):
        if s2 is None:
            self.S.op(eng, lambda e: e.tensor_scalar(out=out, in0=in0, scalar1=s1, scalar2=None, op0=op0), reads=r, writes=w)
        else:
            self.S.op(eng, lambda e: e.tensor_scalar(out=out, in0=in0, scalar1=s1, scalar2=s2, op0=op0, op1=op1), reads=r, writes=w)

    def stt(self, eng, out, in0, scalar, in1, op0, op1, r, w):
        self.S.op(eng, lambda e: e.scalar_tensor_tensor(out=out, in0=in0, scalar=scalar, in1=in1, op0=op0, op1=op1), reads=r, writes=w)

    def cp(self, eng, out, in_, r, w):
        if eng == "act":
            self.S.op("act", lambda e: e.copy(out=out, in_=in_), reads=r, writes=w)
        else:
            self.S.op(eng, lambda e: e.tensor_copy(out=out, in_=in_), reads=r, writes=w)

    def memset(self, eng, ap, val, w):
        self.S.op(eng, lambda e: e.memset(ap, val), writes=w)

    def issue_conv(self, n=None, upto_layer=None):
        q = getattr(self, "conv_q", None)
        if not q:
            return
        k = 0
        while q and (n is None or k < n):
            l2, c, t = q[0]
            if upto_layer is not None and l2 > upto_layer:
                break
            q.pop(0)
            k += 1
            if t == "u":
                self.S.dma("pool", self.Ubf[l2, c], self.I["u_arr"][0][l2, c], writes=[self.b_Ubf[l2][c]])
            else:
                self.S.dma("pool", self.Vbf[l2, c], self.I["v_arr"][0][l2, c], writes=[self.b_Vbf[l2][c]])

    def phase(self):
        self.S.barrier()
        self.off = self.off_global

    def rms(self, x, bx, gb, bgb, out, bout, junk, bjunk, st, bst, valid=None, bvalid=None):
        self.act(junk[:], x[:], AF.Square, [bx], [bjunk, bst], accum_out=st[:, 0:1])
        self.ts("dve", st[:, 1:2], st[:, 0:1], 1.0 / D, 1e-6, ALU.mult, ALU.add, [bst], [bst])
        self.act(st[:, 2:3], st[:, 1:2], AF.Ln, [bst], [bst])
        self.act(st[:, 2:3], st[:, 2:3], AF.Exp, [bst], [bst], scale=-0.5)
        if valid is not None:
            self.tt("dve", st[:, 2:3], st[:, 2:3], valid, ALU.mult, [bst, bvalid], [bst])
        self.stt("dve", out[:], x[:], st[:, 2:3], gb[:], ALU.mult, ALU.mult, [bx, bst, bgb], [bout])

    def transpose_to(self, src, bsrc, n, bank, dst, bdst, evac="act"):
        P = self.P
        psb = P[:, bank, :].bitcast(BF16)
        for c in range(n):
            self.tr(psb[:, c * 128:(c + 1) * 128], src[:, c * 128:(c + 1) * 128], [bsrc], [self.pb[bank]], inc=(c == n - 1))
        self.cp(evac, dst, psb[:, 0:n * 128], [self.pb[bank]], [bdst])

    def build(self):
        cfg = self.cfg
        nc = self.nc
        NT_F, S_P, H = cfg.NT_F, cfg.S_P, cfg.H
        segs = [("f", NT_F)] + ([("p", S_P)] if S_P else [])
        self.es = ExitStack()
        es = self.es
        S = self.S = Sched(nc, es)
        self.off = SB_BASE

        I = {}
        I["xf"] = self.din("xf", [NT_F * 128, D])
        I["memf"] = self.din("memf", [256, D])
        if S_P:
            I["xp"] = self.din("xp", [S_P * 128, D])
            I["memp"] = self.din("memp", [256, D])
            I["validp"] = self.din("validp", [128, S_P])
        I["gains"] = self.din("gains", [DEPTH, 4, 128, D])
        I["gfinal"] = self.din("gfinal", [128, D])
        I["w_in"] = self.din("w_in", [DEPTH, D, 2560])
        I["w_out"] = self.din("w_out", [DEPTH, D, D])
        I["wxq"] = self.din("wxq", [DEPTH, D, 512])
        I["wxk"] = self.din("wxk", [DEPTH, D, 512])
        I["wxv"] = self.din("wxv", [DEPTH, D, 512])
        I["wxo"] = self.din("wxo", [DEPTH, 512, D])
        I["pwq"] = self.din("pwq", [DEPTH, D, 2048])
        I["subkT"] = self.din("subkT", [DEPTH, 128, 16, 128])
        I["u_arr"] = self.din("u_arr", [DEPTH, 128, 128, D])
        I["v_arr"] = self.din("v_arr", [DEPTH, 128, 128, D])
        I["convw"] = self.din("convw", [DEPTH, 128, 6])
        I["sink"] = self.din("sink", [DEPTH, 128, 6])
        I["na_int"] = self.din("na_int", [DEPTH, 128, 6 * 5 * 128])
        I["swa_int"] = self.din("swa_int", [DEPTH, 128, 6 * 3 * 128])
        nfs = len(cfg.f_special())
        I["na_fs"] = self.din("na_fs", [DEPTH, nfs, 128, 6 * 7 * 128])
        I["swa_fs"] = self.din("swa_fs", [DEPTH, 2, 128, 6 * 3 * 128])
        if S_P:
            I["na_p"] = self.din("na_p", [DEPTH, S_P, 128, 6 * 7 * 128])
            I["swa_p"] = self.din("swa_p", [DEPTH, S_P, 128, 6 * 3 * 128])
        self.I = I

        O = {}
        O["yf"] = (nc.dram_tensor("yf", [NT_F * 128, D], F32, kind="ExternalOutput").ap(), [Buf("yf") for _ in range(NT_F)])
        if S_P:
            O["yp"] = (nc.dram_tensor("yp", [cfg.NP_OUT * 128, D], F32, kind="ExternalOutput").ap(), [Buf("yp") for _ in range(cfg.NP_OUT)])
        self.O = O

        SC = {}
        for sg, ns in segs:
            SC[sg] = dict(
                zT=self.dscr("zT" + sg, [16, 128, ns * 128], BF16), b_zT=[Buf("zT") for _ in range(ns)],
                Vs=self.dscr("Vs" + sg, [ns, 128, 640], BF16), b_Vs=[Buf("Vs") for _ in range(ns)],
                x1=self.dscr("x1" + sg, [ns * 128, D], F32), b_x1=[Buf("x1") for _ in range(ns)],
                x2=self.dscr("x2" + sg, [ns * 128, D], F32), b_x2=[Buf("x2") for _ in range(ns)],
                x3=self.dscr("x3" + sg, [ns * 128, D], F32), b_x3=[Buf("x3") for _ in range(ns)],
                xnT=self.dscr("xnT" + sg, [ns, 128, 8, 128], BF16), b_xnT=[Buf("xnT") for _ in range(ns)],
                ixT=self.dscr("ixT" + sg, [ns, 128, 3, 128], BF16), b_ixT=[Buf("ixT") for _ in range(ns)],
                b_zpad=Buf("zpad"),
            )
        self.SC = SC
        self.Ubf = nc.dram_tensor("Ubf", [DEPTH, 128, 128, D], BF16, kind="Internal").ap()
        self.Vbf = nc.dram_tensor("Vbf", [DEPTH, 128, 128, D], BF16, kind="Internal").ap()
        self.b_Ubf = [[Buf("Ubf") for _ in range(128)] for l in range(DEPTH)]
        self.b_Vbf = [[Buf("Vbf") for _ in range(128)] for l in range(DEPTH)]

        self.P = nc.alloc_psum_tensor("P", [128, 8, 512], F32)
        self.pb = [Buf("pb%d" % i) for i in range(8)]
        self.ident, self.b_ident = self.sb("ident", [128, 128], BF16)
        identf, b_identf = self.sb("identf", [128, 128], F32)
        self.iota16, self.b_iota16 = self.sb("iota16", [128, 16], F32)
        self.iota128, self.b_iota128 = self.sb("iota128", [128, 128], F32)
        self.gb, self.b_gb = self.sb("gb", [128, 4, D], F32)
        self.gfin, self.b_gfin = self.sb("gfin", [128, D], F32)
        self.zero, self.b_zero = self.sb("zero", [128, 16], BF16)
        if S_P:
            self.validp, self.b_validp = self.sb("validp", [128, S_P], F32)
        self.off_global = self.off

        S.op("pool", lambda e: e.iota(identf[:], pattern=[[1, 128]], base=0, channel_multiplier=-1, allow_small_or_imprecise_dtypes=True), writes=[b_identf])
        self.ts("dve", self.ident[:], identf[:], 0.0, None, ALU.is_equal, None, [b_identf], [self.b_ident])
        io16 = self.iota16
        io128 = self.iota128
        S.op("pool", lambda e: e.iota(io16[:], pattern=[[1, 16]], base=0, channel_multiplier=0, allow_small_or_imprecise_dtypes=True), writes=[self.b_iota16])
        S.op("pool", lambda e: e.iota(io128[:], pattern=[[1, 128]], base=0, channel_multiplier=0, allow_small_or_imprecise_dtypes=True), writes=[self.b_iota128])
        self.memset("dve", self.zero[:], 0.0, [self.b_zero])
        S.dma("sp", self.gfin[:], I["gfinal"][0], writes=[self.b_gfin])
        if S_P:
            S.dma("sp", self.validp[:], I["validp"][0], writes=[self.b_validp])
        for l in range(DEPTH):
            S.dma("sp", self.gb[:], I["gains"][0][l].rearrange("g p d -> p g d"), writes=[self.b_gb])
            self.phase_z(l)
            if self.stop == "z%d" % l:
                break
            if l == 0 and self.conv:
                self.conv_q = [(l2, c, t) for l2 in range(DEPTH) for c in range(128) for t in ("u", "v")]
            self.phase_mix(l)
            if self.stop == "m%d" % l:
                break
            self.phase_xa(l)
            if self.stop == "x%d" % l:
                break
            self.phase_peer(l)
            if self.stop == "p%d" % l:
                break
        outs = list(O["yf"][1]) + (list(O["yp"][1]) if S_P else [])
        S.wait_all("sp", outs)
        S.barrier()
        S.emit()
        return nc

    def z_slots(self, sg, l):
        cfg = self.cfg
        if sg == "f":
            return list(range(cfg.NT_F))
        return list(range(cfg.S_P)) if l == 0 else list(range(cfg.H, cfg.S_P - cfg.H))

    def mix_slots(self, sg, l):
        cfg = self.cfg
        if sg == "f":
            return list(range(cfg.NT_F))
        return list(range(cfg.H, cfg.S_P - cfg.H)) if l == 0 else list(range(2 * cfg.H, cfg.S_P - 2 * cfg.H))

    def segs(self):
        return ["f"] + (["p"] if self.cfg.S_P else [])

    def x_in(self, sg, l):
        if l == 0:
            return self.I["xf" if sg == "f" else "xp"][0], None
        return self.SC[sg]["x3"], self.SC[sg]["b_x3"]

    def phase_z(self, l):
        S, P, I = self.S, self.P, self.I
        self.phase()
        win, b_win = self.sb("win", [128, 8, 2560], BF16)
        for kc in range(8):
            S.dma("pool", win[:, kc, :], I["w_in"][0][l, kc * 128:(kc + 1) * 128, :], writes=[b_win])
        NB = 3
        xt = [self.sb("xt", [128, D], F32) for _ in range(NB)]
        junks = [self.sb("junk", [128, D], F32) for _ in range(NB)]
        st = [self.sb("st", [128, 4], F32) for _ in range(NB)]
        hb = [self.sb("hb", [128, D], BF16) for _ in range(NB)]
        hT = [self.sb("hT", [128, 8, 128], BF16) for _ in range(NB)]
        zs = [self.sb("zs", [128, 16, 128], BF16) for _ in range(NB)]
        vs = [self.sb("vs", [128, 8, 80], BF16) for _ in range(NB)]
        for i in range(NB):
            self.memset("dve", vs[i][0][:], 0.0, [vs[i][1]])
            self.memset("dve", vs[i][0][:, :, 64:65], 1.0, [vs[i][1]])
        it = 0
        BIS = int(os.environ.get("BIS", "99"))
        if BIS == 0:
            return
        zlist = [(sg, s) for sg in self.segs() for s in self.z_slots(sg, l)]
        for sg in self.segs():
            sc = self.SC[sg]
            xin, bxin = self.x_in(sg, l)
            for s in self.z_slots(sg, l):
                b = it % NB
                par = it % 2
                junk, b_junk = junks[b]
                x, bx = xt[b]
                if it == 0:
                    S.dma("sp", x[:], xin[s * 128:(s + 1) * 128, :], reads=([bxin[s]] if bxin else []), writes=[bx])
                if it + 1 < len(zlist):
                    sg2, s2 = zlist[it + 1]
                    xin2, bxin2 = self.x_in(sg2, l)
                    x2_, bx2_ = xt[(it + 1) % NB]
                    S.dma("sp", x2_[:], xin2[s2 * 128:(s2 + 1) * 128, :], reads=([bxin2[s2]] if bxin2 else []), writes=[bx2_])
                it += 1
                h, bh = hb[b]
                if sg == "p":
                    self.rms(x, bx, self.gb[:, 0, :], self.b_gb, h, bh, junk, b_junk, st[b][0], st[b][1],
                             valid=self.validp[:, s:s + 1], bvalid=self.b_validp)
                else:
                    self.rms(x, bx, self.gb[:, 0, :], self.b_gb, h, bh, junk, b_junk, st[b][0], st[b][1])
                if BIS == 1:
                    continue
                t, bt = hT[b]
                self.transpose_to(h, bh, 8, 4 * par, t[:].rearrange("p c t -> p (c t)"), bt)
                if BIS == 2:
                    continue
                z, bz = zs[b]
                for g in range(4):
                    bank = 4 * par + 1 + (g % 2)
                    for j in range(4):
                        c = g * 4 + j
                        for kc in range(8):
                            self.mm(P[:, bank, j * 128:(j + 1) * 128], win[:, kc, c * 128:(c + 1) * 128], t[:, kc, :],
                                    kc == 0, kc == 7, [b_win, bt], [self.pb[bank]], inc=(j == 3 and kc == 7))
                    self.cp("act" if g % 2 == 0 else "dve", z[:, g * 4:(g + 1) * 4, :].rearrange("p c t -> p (c t)"), P[:, bank, :], [self.pb[bank]], [bz])
                if BIS == 3:
                    continue
                v, bv = vs[b]
                vbk = 4 * par + 3
                for kc in range(8):
                    self.mm(P[:, vbk, :], t[:, kc, :], win[:, kc, 2048:2560], kc == 0, kc == 7, [b_win, bt], [self.pb[vbk]], inc=(kc == 7))
                self.cp("dve", v[:, :, 0:64], P[:, vbk, :].rearrange("p (h d) -> p h d", d=64), [self.pb[vbk]], [bv])
                S.dma("sp", sc["zT"][:, :, s * 128:(s + 1) * 128].rearrange("c p t -> p c t"), z[:], reads=[bz], writes=[sc["b_zT"][s]])
                S.dma("sp", sc["Vs"][s], v[:].rearrange("p h d -> p (h d)"), reads=[bv], writes=[sc["b_Vs"][s]])

    def na_keys(self, sg, s):
        cfg = self.cfg
        if sg == "f":
            k0, k1 = na_key_range(s, cfg.NT_F)
            spec = cfg.f_special()
            return k0, k1, (spec.index(s) if s in spec else None)
        return s - 3, s + 3, s

    def swa_keys(self, sg, s):
        cfg = self.cfg
        if sg == "f":
            k0, k1 = max(s - 1, 0), min(s + 1, cfg.NT_F - 1)
            spec = cfg.f_swa_special()
            return k0, k1, (spec.index(s) if s in spec else None)
        return s - 1, s + 1, s

    def _mix_loads(self, S, I, l, sg, s, sc, zT, Vs, xin, bxin, k0, k1, nk, w0, w1, nw, nsp, ssp, c0, zdeps, vdeps,
                   q_n, bqn, q_s, bqs, sc_, bsc, k_n, bkn, v_n, bvn, k_s, bks, v_w, bvw, x, bx, qn4, nasp, b_nasp, swsp, b_swsp):
        S.dma("sp", qn4[0:64, :, 0, :], zT[0:3, 0:64, c0:c0 + 128].rearrange("c p t -> p c t"), reads=zdeps, writes=[bqn])
        S.dma("sp", qn4[64:128, :, 1, :], zT[0:3, 64:128, c0:c0 + 128].rearrange("c p t -> p c t"), reads=zdeps, writes=[bqn])
        S.dma("sp", k_n[:, :, 0:nk * 128], zT[3:6, :, k0 * 128:(k1 + 1) * 128].rearrange("c p t -> p c t"), reads=zdeps, writes=[bkn])
        S.dma("sp", v_n[:, 0:nk, :], Vs[k0:k1 + 1, :, 0:480].rearrange("k p f -> p k f"), reads=vdeps, writes=[bvn])
        if nsp is not None:
            src = I["na_fs"][0][l, nsp] if sg == "f" else I["na_p"][0][l, s]
            S.dma("sp", nasp[:, 0:6 * nk * 128], src[:, 0:6 * nk * 128], writes=[b_nasp])
        S.dma("sp", q_s[0:64, 0:3, :], zT[12:15, 0:64, c0:c0 + 128].rearrange("c p t -> p c t"), reads=zdeps, writes=[bqs])
        S.dma("sp", q_s[64:128, 3:6, :], zT[12:15, 64:128, c0:c0 + 128].rearrange("c p t -> p c t"), reads=zdeps, writes=[bqs])
        S.dma("sp", k_s[:, 0:nw * 128], zT[15, :, w0 * 128:(w1 + 1) * 128], reads=zdeps, writes=[bks])
        S.dma("sp", v_w[:, 0:nw, :], Vs[w0:w1 + 1, :, 480:640].rearrange("k p f -> p k f"), reads=vdeps, writes=[bvw])
        if ssp is not None:
            src = I["swa_fs"][0][l, ssp] if sg == "f" else I["swa_p"][0][l, s]
            S.dma("sp", swsp[:, 0:6 * nw * 128], src[:, 0:6 * nw * 128], writes=[b_swsp])
        lo = 1 if (sg == "f" and s == 0) else 0
        hi = 1 if (sg == "f" and s == self.cfg.NT_F - 1) else 0
        S.dma("sp", sc_[:, :, lo:130 - hi], zT[6:12, :, c0 - 1 + lo:c0 + 129 - hi].rearrange("c p t -> p c t"), reads=zdeps, writes=[bsc])
        if lo:
            self.memset("dve", sc_[:, :, 0:1], 0.0, [bsc])
        if hi:
            self.memset("dve", sc_[:, :, 129:130], 0.0, [bsc])
        S.dma("sp", x[:], xin[s * 128:(s + 1) * 128, :], reads=([bxin[s]] if bxin else []), writes=[bx])


    def phase_mix(self, l):
        S, P, I, pb = self.S, self.P, self.I, self.pb
        self.phase()
        wout, b_wout = self.sb("wout", [128, 8, D], BF16)
        for kc in range(8):
            S.dma("pool", wout[:, kc, :], I["w_out"][0][l, kc * 128:(kc + 1) * 128, :], writes=[b_wout])
        naint, b_naint = self.sb("naint", [128, 6 * 5 * 128], F32)
        swint, b_swint = self.sb("swint", [128, 6 * 3 * 128], F32)
        S.dma("sp", naint[:], I["na_int"][0][l], writes=[b_naint])
        S.dma("sp", swint[:], I["swa_int"][0][l], writes=[b_swint])
        naspb = [self.sb("nasp", [128, 6 * 7 * 128], F32) for _ in range(2)]
        swspb = [self.sb("swsp", [128, 6 * 3 * 128], F32) for _ in range(2)]
        cw, b_cw = self.sb("cw", [128, 6], F32)
        S.dma("sp", cw[:], I["convw"][0][l], writes=[b_cw])
        es_, b_es = self.sb("es", [128, 6], F32)
        S.dma("sp", es_[:], I["sink"][0][l], writes=[b_es])
        self.act(es_[:], es_[:], AF.Exp, [b_es], [b_es])
        NB = 3
        qn = [self.sb("qn", [128, 6, 128], BF16) for _ in range(NB)]
        qs = [self.sb("qs", [128, 6, 128], BF16) for _ in range(NB)]
        for i in range(NB):
            self.memset("dve", qn[i][0][:], 0.0, [qn[i][1]])
            self.memset("dve", qs[i][0][:], 0.0, [qs[i][1]])
        scb = [self.sb("scb", [128, 6, 130], BF16) for _ in range(NB)]
        kn = [self.sb("kn", [128, 3, 7 * 128], BF16) for _ in range(NB)]
        vn = [self.sb("vn", [128, 7, 480], BF16) for _ in range(NB)]
        ks = [self.sb("ks", [128, 3 * 128], BF16) for _ in range(NB)]
        vw = [self.sb("vw", [128, 3, 160], BF16) for _ in range(NB)]
        xt = [self.sb("xt", [128, D], F32) for _ in range(NB)]
        sgb = [self.sb("sg", [128, 512], F32) for _ in range(2)]
        ptb = [self.sb("pt", [128, 512], BF16) for _ in range(2)]
        yb = [self.sb("y", [128, 768], BF16)] * 2
        yTb = [self.sb("yT", [128, 6, 128], BF16)] * 2
        yscb = [self.sb("ysc", [128, 2, 128], BF16)] * 2
        ub_ = [self.sb("u", [128, 2, 130], F32)] * 2
        t1b = [self.sb("t1", [128, 2, 128], F32)] * 2
        t2b = [self.sb("t2", [128, 2, 128], F32)] * 2
        rdb = [self.sb("rd", [128, 12], F32)] * 2
        x1b = [self.sb("x1", [128, D], F32)] * 2
        gi = [0]

        def attn(nk, kfn, qfn, vfn, bias, b_bias, po, rbufs):
            pairs = [(h, ki) for h in range(6) for ki in range(nk)]
            for g0 in range(0, len(pairs), 4):
                grp = pairs[g0:g0 + 4]
                n = len(grp)
                k = gi[0] % 2
                k3 = gi[0] % 2
                gi[0] += 1
                bank = k
                for j, (h, ki) in enumerate(grp):
                    self.mm(P[:, bank, j * 128:(j + 1) * 128], kfn(h, ki), qfn(h), True, True, rbufs, [pb[bank]], inc=(j == n - 1))
                sg_, bsg = sgb[k3]
                pt, bpt = ptb[k3]
                self.stt("dve", sg_[:, 0:n * 128], P[:, bank, 0:n * 128], 0.125, bias[:, g0 * 128:(g0 + n) * 128], ALU.mult, ALU.add, [pb[bank], b_bias], [bsg])
                self.act(pt[:, 0:n * 128], sg_[:, 0:n * 128], AF.Exp, [bsg], [bpt])
                for j, (h, ki) in enumerate(grp):
                    self.mm(P[:, po, h * 80:h * 80 + 65], pt[:, j * 128:(j + 1) * 128], vfn(h, ki), ki == 0, ki == nk - 1, [bpt] + rbufs, [pb[po]], inc=(j == n - 1))

        spb = {}

        def front(sg, s, b, b3, do_loads, do_compute):
            sc = self.SC[sg]
            zT, Vs = sc["zT"], sc["Vs"]
            xin, bxin = self.x_in(sg, l)
            k0, k1, nsp = self.na_keys(sg, s)
            nk = k1 - k0 + 1
            w0, w1, ssp = self.swa_keys(sg, s)
            nw = w1 - w0 + 1
            c0 = s * 128
            zdeps = [sc["b_zT"][k] for k in range(k0, k1 + 1)]
            vdeps = [sc["b_Vs"][k] for k in range(k0, k1 + 1)]
            (q_n, bqn), (q_s, bqs), (sc_, bsc) = qn[b3], qs[b3], scb[b3]
            (k_n, bkn), (v_n, bvn), (k_s, bks), (v_w, bvw) = kn[b3], vn[b3], ks[b3], vw[b3]
            x, bx = xt[b3]
            qn4 = q_n[:].rearrange("p (c two) t -> p c two t", two=2)
            (nasp, b_nasp), (swsp, b_swsp) = naspb[b], swspb[b]
            if nsp is None:
                nab, bnab = naint, b_naint
            else:
                nab, bnab = nasp, b_nasp
            if ssp is None:
                swb, bswb = swint, b_swint
            else:
                swb, bswb = swsp, b_swsp
            if do_loads:
                self._mix_loads(S, I, l, sg, s, sc, zT, Vs, xin, bxin, k0, k1, nk, w0, w1, nw, nsp, ssp, c0, zdeps, vdeps,
                                q_n, bqn, q_s, bqs, sc_, bsc, k_n, bkn, v_n, bvn, k_s, bks, v_w, bvw, x, bx, qn4, nasp, b_nasp, swsp, b_swsp)
            if not do_compute:
                return
            po = 2 + 2 * b
            attn(nk,
                 lambda h, ki: k_n[:, h // 2, ki * 128:(ki + 1) * 128],
                 lambda h: q_n[:, h, :],
                 lambda h, ki: v_n[:, ki, h * 80:h * 80 + 65],
                 nab, bnab, po, [bkn, bqn, bvn])
            attn(nw,
                 lambda h, ki: k_s[:, ki * 128:(ki + 1) * 128],
                 lambda h: q_s[:, h, :],
                 lambda h, ki: v_w[:, ki, (h // 3) * 80:(h // 3) * 80 + 65],
                 swb, bswb, po + 1, [bks, bqs, bvw])

        def tail(sg, s, b, b3):
            sc = self.SC[sg]
            (sc_, bsc) = scb[b3]
            x, bx = xt[b3]
            (y, b_y), (yT, b_yT), (ysc, b_ysc) = yb[b], yTb[b], yscb[b]
            (u_, b_u), (t1, b_t1), (t2, b_t2), (rd, b_rd), (x1, b_x1t) = ub_[b], t1b[b], t2b[b], rdb[b], x1b[b]
            po = 2 + 2 * b
            pna = P[:, po, 0:480].rearrange("p (h d) -> p h d", d=80)
            psw = P[:, po + 1, 0:480].rearrange("p (h d) -> p h d", d=80)
            self.cp("dve", rd[:, 0:6], pna[:, :, 64], [pb[po]], [b_rd])
            self.tt("dve", rd[:, 6:12], psw[:, :, 64], es_[:], ALU.add, [pb[po + 1], b_es], [b_rd])
            S.op("dve", lambda e: e.reciprocal(out=rd[:], in_=rd[:]), reads=[b_rd], writes=[b_rd])
            self.tt("dve", y[:, 0:384].rearrange("p (h d) -> p h d", d=64), pna[:, :, 0:64], rd[:, 0:6].unsqueeze(2).to_broadcast([128, 6, 64]), ALU.mult, [pb[po], b_rd], [b_y])
            self.tt("dve", y[:, 384:768].rearrange("p (h d) -> p h d", d=64), psw[:, :, 0:64], rd[:, 6:12].unsqueeze(2).to_broadcast([128, 6, 64]), ALU.mult, [pb[po + 1], b_rd], [b_y])
            self.transpose_to(y, b_y, 6, 6, yT[:].rearrange("p c t -> p (c t)"), b_yT, evac="act")
            self.tt("dve", u_[:], sc_[:, 2:4, :], sc_[:, 4:6, :], ALU.mult, [bsc], [b_u])
            for c in range(2):
                self.ts("dve", t1[:, c, :], u_[:, c, 0:128], cw[:, c:c + 1], None, ALU.mult, None, [b_u, b_cw], [b_t1])
                for tap in (1, 2):
                    self.ts("dve", t2[:, c, :], u_[:, c, tap:tap + 128], cw[:, 2 * tap + c:2 * tap + c + 1], None, ALU.mult, None, [b_u, b_cw], [b_t2])
                    self.tt("dve", t1[:, c, :], t1[:, c, :], t2[:, c, :], ALU.add, [b_t1, b_t2], [b_t1])
            self.tt("dve", ysc[:], t1[:], sc_[:, 0:2, 1:129], ALU.mult, [b_t1, bsc], [b_ysc])
            lhs = [(yT[:, 0, :], b_yT), (yT[:, 1, :], b_yT), (yT[:, 2, :], b_yT), (ysc[:, 0, :], b_ysc), (ysc[:, 1, :], b_ysc),
                   (yT[:, 3, :], b_yT), (yT[:, 4, :], b_yT), (yT[:, 5, :], b_yT)]
            for hf in range(2):
                bank = 7 - hf
                for c in range(8):
                    self.mm(P[:, bank, :], lhs[c][0], wout[:, c, hf * 512:(hf + 1) * 512], c == 0, c == 7, [lhs[c][1], b_wout], [pb[bank]], inc=(c == 7))
                self.tt("dve", x1[:, hf * 512:(hf + 1) * 512], P[:, bank, :], x[:, hf * 512:(hf + 1) * 512], ALU.add, [pb[bank], bx], [b_x1t])
            S.dma("sp", sc["x1"][s * 128:(s + 1) * 128, :], x1[:], reads=[b_x1t], writes=[sc["b_x1"][s]])

        slots = [(sg, s) for sg in self.segs() for s in self.mix_slots(sg, l)]
        prev = []
        if slots:
            front(slots[0][0], slots[0][1], 0, 0, True, False)
        for it, (sg, s) in enumerate(slots):
            b = it % 2
            b3 = it % 3
            if it + 1 < len(slots):
                front(slots[it + 1][0], slots[it + 1][1], (it + 1) % 2, (it + 1) % 3, True, False)
            S.begin_capture()
            front(sg, s, b, b3, False, True)
            F = S.end_capture()
            S.begin_capture()
            tail(sg, s, b, b3)
            T = S.end_capture()
            S.feed_merged(F, prev)
            prev = T
            self.issue_conv(2)
        S.feed(prev)

    def phase_xa(self, l):
        S, P, I, pb = self.S, self.P, self.I, self.pb
        self.phase()
        wxq, b_wxq = self.sb("wxq", [128, 8, 512], BF16)
        wxo, b_wxo = self.sb("wxo", [128, 4, D], BF16)
        pwq, b_pwq = self.sb("pwq", [128, 8, 2048], BF16)
        subk, b_subk = self.sb("subk", [128, 16, 128], BF16)
        for kc in range(8):
            S.dma("pool", wxq[:, kc, :], I["wxq"][0][l, kc * 128:(kc + 1) * 128, :], writes=[b_wxq])
            S.dma("pool", pwq[:, kc, :], I["pwq"][0][l, kc * 128:(kc + 1) * 128, :], writes=[b_pwq])
        for kc in range(4):
            S.dma("pool", wxo[:, kc, :], I["wxo"][0][l, kc * 128:(kc + 1) * 128, :], writes=[b_wxo])
        S.dma("pool", subk[:], I["subkT"][0][l], writes=[b_subk])
        kmT = {}
        vm = {}
        for sg in self.segs():
            kmT[sg] = self.sb("kmT", [128, 4, 256], BF16)
            vm[sg] = self.sb("vm", [128, 2, 4, 144], BF16)
            self.memset("dve", vm[sg][0][:], 0.0, [vm[sg][1]])
            self.memset("dve", vm[sg][0][:, :, :, 128:129], 1.0, [vm[sg][1]])
        NB = 2
        xt = [self.sb("xt", [128, D], F32) for _ in range(NB)]
        junk, b_junk = self.sb("junk", [128, D], F32)
        st, b_st = self.sb("st", [128, 4], F32)
        xn, b_xn = self.sb("xn", [128, D], BF16)
        xnT, b_xnT = self.sb("xnT", [128, 8, 128], BF16)
        sc_, b_sc = self.sb("scores", [128, 16, 128], F32)
        cand, b_cand = self.sb("cand", [128, 8, 256], F32)
        b_scg = [Buf("scg") for _ in range(16)]
        b_cag = [Buf("cag") for _ in range(8)]
        b_tsg = [Buf("tsg") for _ in range(16)]
        b_tig = [Buf("tig") for _ in range(16)]
        b_bsg = [Buf("bsg") for _ in range(8)]
        b_bpg = [Buf("bpg") for _ in range(8)]
        qx, b_qx = self.sb("qx", [128, 4, 128], BF16)
        ptx, b_ptx = self.sb("ptx", [128, 8, 128], BF16)
        rd, b_rd = self.sb("rd", [128, 4], F32)
        o_, b_o = self.sb("o", [128, 512], BF16)
        oT, b_oT = self.sb("oT", [128, 4, 128], BF16)
        x2, b_x2t = self.sb("x2", [128, D], F32)
        x3n, b_x3n = self.sb("x3n", [128, D], BF16)
        x3T, b_x3T = self.sb("x3T", [128, 8, 128], BF16)
        qT, b_qT = self.sb("qT", [128, 16, 128], BF16)
        tsv, b_tsv = self.sb("tsv", [128, 16, 16], F32)
        tiv, b_tiv = self.sb("tiv", [128, 16, 16], U32)
        tif, b_tif = self.sb("tif", [128, 16, 16], F32)
        bsv, b_bsv = self.sb("bsv", [128, 8, 16], F32)
        bpv, b_bpv = self.sb("bpv", [128, 8, 16], U32)
        k12u, b_k12u = self.sb("k12u", [128, 2, 128], U32)
        k12f, b_k12f = self.sb("k12f", [128, 2, 128], F32)
        ig, b_ig = self.sb("ig", [128, 3, 128], BF16)
        igf, b_igf = self.sb("igf", [128, 3, 128], F32)
        gsm, b_gsm = self.sb("gsm", [128, 8], F32)
        ge, b_ge = self.sb("ge", [128, 8, 16], F32)
        igT, b_igT = self.sb("igT", [128, 3, 128], BF16)

        wk = sc_[:].rearrange("p a b -> p (a b)").bitcast(BF16).rearrange("p (k n) -> p k n", k=8)
        wv = cand[:].rearrange("p a b -> p (a b)").bitcast(BF16).rearrange("p (k n) -> p k n", k=8)
        for kc in range(8):
            S.dma("pool", wk[:, kc, :], I["wxk"][0][l, kc * 128:(kc + 1) * 128, :], writes=[b_sc])
            S.dma("pool", wv[:, kc, :], I["wxv"][0][l, kc * 128:(kc + 1) * 128, :], writes=[b_cand])
        memT, b_memT = qT, b_qT
        for sg in self.segs():
            mem = I["memf" if sg == "f" else "memp"][0]
            for mt in range(2):
                x, bx = xt[mt]
                S.dma("sp", x[:], mem[mt * 128:(mt + 1) * 128, :], writes=[bx])
                self.rms(x, bx, self.gb[:, 2, :], self.b_gb, xn, b_xn, junk, b_junk, st, b_st)
                self.transpose_to(xn, b_xn, 8, 0, memT[:, mt * 8:(mt + 1) * 8, :].rearrange("p c t -> p (c t)"), b_memT)
            (km, bkm), (vmm, bvm) = kmT[sg], vm[sg]
            for hh in range(2):
                for h2 in range(2):
                    h = hh * 2 + h2
                    for mt in range(2):
                        for kc in range(8):
                            self.mm(P[:, 1, (h2 * 2 + mt) * 128:(h2 * 2 + mt + 1) * 128], wk[:, kc, h * 128:(h + 1) * 128], memT[:, mt * 8 + kc, :],
                                    kc == 0, kc == 7, [b_sc, b_memT], [pb[1]], inc=(kc == 7 and mt == 1 and h2 == 1))
                self.cp("act", km[:, hh * 2:(hh + 1) * 2, :].rearrange("p h m -> p (h m)"), P[:, 1, :], [pb[1]], [bkm])
            for mt in range(2):
                for kc in range(8):
                    self.mm(P[:, 2, :], memT[:, mt * 8 + kc, :], wv[:, kc, :], kc == 0, kc == 7, [b_cand, b_memT], [pb[2]], inc=(kc == 7))
                self.cp("dve", vmm[:, mt, :, 0:128], P[:, 2, :].rearrange("p (h d) -> p h d", d=128), [pb[2]], [bvm])

        TT = [dict(sc=sc_, b_sc=b_sc, cand=cand, b_cand=b_cand, b_scg=b_scg, b_cag=b_cag, b_tsg=b_tsg, b_tig=b_tig, b_bsg=b_bsg, b_bpg=b_bpg,
                   tsv=tsv, tiv=tiv, tif=tif, b_tif=b_tif, bsv=bsv, bpv=bpv, k12u=k12u, b_k12u=b_k12u, k12f=k12f, b_k12f=b_k12f,
                   ig=ig, b_ig=b_ig, igf=igf, b_igf=b_igf, gsm=gsm, b_gsm=b_gsm, ge=ge, b_ge=b_ge, igT=igT, b_igT=b_igT)]
        d = {}
        d["sc"], d["b_sc"] = self.sb("scores", [128, 16, 128], F32)
        d["cand"], d["b_cand"] = self.sb("cand", [128, 8, 256], F32)
        for nm, n in (("b_scg", 16), ("b_cag", 8), ("b_tsg", 16), ("b_tig", 16), ("b_bsg", 8), ("b_bpg", 8)):
            d[nm] = [Buf(nm) for _ in range(n)]
        d["tsv"], _ = self.sb("tsv", [128, 16, 16], F32)
        d["tiv"], _ = self.sb("tiv", [128, 16, 16], U32)
        d["tif"], d["b_tif"] = self.sb("tif", [128, 16, 16], F32)
        d["bsv"], _ = self.sb("bsv", [128, 8, 16], F32)
        d["bpv"], _ = self.sb("bpv", [128, 8, 16], U32)
        d["k12u"], d["b_k12u"] = self.sb("k12u", [128, 2, 128], U32)
        d["k12f"], d["b_k12f"] = self.sb("k12f", [128, 2, 128], F32)
        d["ig"], d["b_ig"] = self.sb("ig", [128, 3, 128], BF16)
        d["igf"], d["b_igf"] = self.sb("igf", [128, 3, 128], F32)
        d["gsm"], d["b_gsm"] = self.sb("gsm", [128, 8], F32)
        d["ge"], d["b_ge"] = self.sb("ge", [128, 8, 16], F32)
        d["igT"], d["b_igT"] = self.sb("igT", [128, 3, 128], BF16)
        TT.append(d)

        def front(sg, s, b):
            sc = self.SC[sg]
            (km, bkm), (vmm, bvm) = kmT[sg], vm[sg]
            T = TT[b]
            x, bx = xt[b]
            S.dma("sp", x[:], sc["x1"][s * 128:(s + 1) * 128, :], reads=[sc["b_x1"][s]], writes=[bx])
            self.rms(x, bx, self.gb[:, 1, :], self.b_gb, xn, b_xn, junk, b_junk, st, b_st)
            self.transpose_to(xn, b_xn, 8, 0, xnT[:].rearrange("p c t -> p (c t)"), b_xnT)
            for h in range(4):
                for kc in range(8):
                    self.mm(P[:, 1, h * 128:(h + 1) * 128], wxq[:, kc, h * 128:(h + 1) * 128], xnT[:, kc, :], kc == 0, kc == 7,
                            [b_wxq, b_xnT], [pb[1]], inc=(kc == 7 and h == 3))
            self.cp("act", qx[:].rearrange("p h t -> p (h t)"), P[:, 1, :], [pb[1]], [b_qx])
            for hh in range(2):
                bank = 2 + hh
                for h2 in range(2):
                    h = hh * 2 + h2
                    for mt in range(2):
                        j = h2 * 2 + mt
                        self.mm(P[:, bank, j * 128:(j + 1) * 128], km[:, h, mt * 128:(mt + 1) * 128], qx[:, h, :], True, True,
                                [bkm, b_qx], [pb[bank]], inc=(j == 3))
                self.act(ptx[:, hh * 4:(hh + 1) * 4, :].rearrange("p a t -> p (a t)"), P[:, bank, :], AF.Exp, [pb[bank]], [b_ptx], scale=float(128 ** -0.5))
            for hh in range(2):
                bank = 4 if hh == 0 else 6
                for h2 in range(2):
                    h = hh * 2 + h2
                    for mt in range(2):
                        self.mm(P[:, bank, h2 * 144:h2 * 144 + 129], ptx[:, h * 2 + mt, :], vmm[:, mt, h, 0:129], mt == 0, mt == 1,
                                [b_ptx, bvm], [pb[bank]], inc=(mt == 1 and h2 == 1))
                pv = P[:, bank, 0:288].rearrange("p (h d) -> p h d", d=144)
                S.op("dve", lambda e, pv=pv, hh=hh: e.reciprocal(out=rd[:, hh * 2:(hh + 1) * 2], in_=pv[:, :, 128]), reads=[pb[bank]], writes=[b_rd])
                self.tt("dve", o_[:, hh * 256:(hh + 1) * 256].rearrange("p (h d) -> p h d", d=128), pv[:, :, 0:128],
                        rd[:, hh * 2:(hh + 1) * 2].unsqueeze(2).to_broadcast([128, 2, 128]), ALU.mult, [pb[bank], b_rd], [b_o])
            self.transpose_to(o_, b_o, 4, 0, oT[:].rearrange("p c t -> p (c t)"), b_oT)
            for hf in range(2):
                for c in range(4):
                    self.mm(P[:, 6 + hf, :], oT[:, c, :], wxo[:, c, hf * 512:(hf + 1) * 512], c == 0, c == 3, [b_oT, b_wxo], [pb[6 + hf]], inc=(c == 3))
            self.tt("dve", x2[:].rearrange("p (a d) -> p a d", a=2), P[:, 6:8, :], x[:].rearrange("p (a d) -> p a d", a=2), ALU.add, [pb[6], pb[7], bx], [b_x2t])
            S.dma("sp", sc["x2"][s * 128:(s + 1) * 128, :], x2[:], reads=[b_x2t], writes=[sc["b_x2"][s]])
            self.rms(x2, b_x2t, self.gb[:, 3, :], self.b_gb, x3n, b_x3n, junk, b_junk, st, b_st)
            self.transpose_to(x3n, b_x3n, 8, 0, x3T[:].rearrange("p c t -> p (c t)"), b_x3T)
            S.dma("sp", sc["xnT"][s], x3T[:], reads=[b_x3T], writes=[sc["b_xnT"][s]])
            for g in range(4):
                bank = 1 + (g % 2)
                for j in range(4):
                    c = g * 4 + j
                    for kc in range(8):
                        self.mm(P[:, bank, j * 128:(j + 1) * 128], pwq[:, kc, c * 128:(c + 1) * 128], x3T[:, kc, :], kc == 0, kc == 7,
                                [b_pwq, b_x3T], [pb[bank]], inc=(kc == 7 and j == 3))
                self.cp("act", qT[:, g * 4:(g + 1) * 4, :].rearrange("p c t -> p (c t)"), P[:, bank, :], [pb[bank]], [b_qT])
            for g in range(4):
                bank = 3 + (g % 2)
                for j in range(4):
                    c = g * 4 + j
                    self.mm(P[:, bank, j * 128:(j + 1) * 128], qT[:, c, :], subk[:, c, :], True, True, [b_qT, b_subk], [pb[bank]], inc=(j == 3))
                self.cp("act", T["sc"][:, g * 4:(g + 1) * 4, :].rearrange("p c n -> p (c n)"), P[:, bank, :], [pb[bank]], [T["b_sc"]] + T["b_scg"][g * 4:(g + 1) * 4])

        def tail(sg, s, b):
            sc = self.SC[sg]
            T = TT[b]
            sc_t, cand_t, tsv_t, tiv_t, tif_t, bsv_t, bpv_t = T["sc"], T["cand"], T["tsv"], T["tiv"], T["tif"], T["bsv"], T["bpv"]
            k12u_t, k12f_t, ig_t, igf_t, gsm_t, ge_t, igT_t = T["k12u"], T["k12f"], T["ig"], T["igf"], T["gsm"], T["ge"], T["igT"]
            self.top16(sc_t, T["b_scg"], 16, tsv_t, T["b_tsg"], tiv_t, T["b_tig"])
            self.cp("dve", tif_t[:], tiv_t[:], T["b_tig"], [T["b_tif"]])
            ts4 = tsv_t[:].rearrange("p (h a) k -> p h a k", a=2)
            self.tt("dve", cand_t[:].rearrange("p h (a b) -> p h a b", b=16),
                    ts4[:, :, 0, :].unsqueeze(3).to_broadcast([128, 8, 16, 16]),
                    ts4[:, :, 1, :].unsqueeze(2).to_broadcast([128, 8, 16, 16]), ALU.add, T["b_tsg"], [T["b_cand"]] + T["b_cag"])
            self.top16(cand_t, T["b_cag"], 8, bsv_t, T["b_bsg"], bpv_t, T["b_bpg"])
            self.ts("dve", k12u_t[:, 0, :], bpv_t[:].rearrange("p h k -> p (h k)"), 4, None, ALU.arith_shift_right, None, T["b_bpg"], [T["b_k12u"]])
            self.ts("dve", k12u_t[:, 1, :], bpv_t[:].rearrange("p h k -> p (h k)"), 15, None, ALU.bitwise_and, None, T["b_bpg"], [T["b_k12u"]])
            self.cp("dve", k12f_t[:], k12u_t[:], [T["b_k12u"]], [T["b_k12f"]])
            tif4 = tif_t[:].rearrange("p (h a) k -> p h a k", a=2)
            oh = sc_t[:].rearrange("p a b -> p (a b)").rearrange("p (h k j) -> p h k j", h=8, k=16)
            for a in range(2):
                self.tt("dve", oh, k12f_t[:, a, :].rearrange("p (h k) -> p h k", h=8).unsqueeze(3).to_broadcast([128, 8, 16, 16]),
                        self.iota16[:].unsqueeze(1).unsqueeze(1).to_broadcast([128, 8, 16, 16]), ALU.is_equal, [T["b_k12f"], self.b_iota16], [T["b_sc"]] + T["b_scg"])
                self.tt("dve", oh, oh, tif4[:, :, a, :].unsqueeze(2).to_broadcast([128, 8, 16, 16]), ALU.mult, [T["b_sc"], T["b_tif"]], [T["b_sc"]])
                S.op("dve", lambda e, a=a, oh=oh, igf_t=igf_t: e.tensor_reduce(out=igf_t[:, a, :], in_=oh.rearrange("p h k j -> p (h k) j"), axis=AX.X, op=ALU.add), reads=[T["b_sc"]], writes=[T["b_igf"]])
            self.tt("dve", ge_t[:], bsv_t[:], bsv_t[:, :, 0:1].to_broadcast([128, 8, 16]), ALU.subtract, T["b_bsg"], [T["b_ge"]])
            self.act(ge_t[:], ge_t[:], AF.Exp, [T["b_ge"]], [T["b_ge"]])
            S.op("dve", lambda e, gsm_t=gsm_t, ge_t=ge_t: e.tensor_reduce(out=gsm_t[:], in_=ge_t[:], axis=AX.X, op=ALU.add), reads=[T["b_ge"]], writes=[T["b_gsm"]])
            S.op("dve", lambda e, gsm_t=gsm_t: e.reciprocal(out=gsm_t[:], in_=gsm_t[:]), reads=[T["b_gsm"]], writes=[T["b_gsm"]])
            self.tt("dve", igf_t[:, 2, :].rearrange("p (h k) -> p h k", h=8), ge_t[:], gsm_t[:].unsqueeze(2).to_broadcast([128, 8, 16]), ALU.mult, [T["b_ge"], T["b_gsm"]], [T["b_igf"]])
            self.cp("dve", ig_t[:], igf_t[:], [T["b_igf"]], [T["b_ig"]])
            self.transpose_to(ig_t[:].rearrange("p a t -> p (a t)"), T["b_ig"], 3, 5, igT_t[:].rearrange("p c t -> p (c t)"), T["b_igT"])
            S.dma("sp", sc["ixT"][s], igT_t[:], reads=[T["b_igT"]], writes=[sc["b_ixT"][s]])

        slots = [(sg, s) for sg in self.segs() for s in self.mix_slots(sg, l)]
        prev = []
        for it, (sg, s) in enumerate(slots):
            b = it % 2
            S.begin_capture()
            front(sg, s, b)
            F = S.end_capture()
            S.begin_capture()
            tail(sg, s, b)
            Tl = S.end_capture()
            S.feed_merged(F, prev)
            prev = Tl
            self.issue_conv(2)
        S.feed(prev)

    def top16(self, src, bsrc, ngrp, vals, bvals, idx, bidx):
        S = self.S
        for g in range(ngrp):
            sl = src[:, g, :]
            S.op("dve", lambda e, g=g, sl=sl: e.max(out=vals[:, g, 0:8], in_=sl), reads=[bsrc[g]], writes=[bvals[g]])
        for g in range(ngrp):
            sl = src[:, g, :]
            S.op("dve", lambda e, g=g, sl=sl: e.max_index(out=idx[:, g, 0:8], in_max=vals[:, g, 0:8], in_values=sl), reads=[bsrc[g], bvals[g]], writes=[bidx[g]])
        for g in range(ngrp):
            sl = src[:, g, :]
            S.op("dve", lambda e, g=g, sl=sl: e.match_replace(out=sl, in_to_replace=vals[:, g, 0:8], in_values=sl, imm_value=NEG), reads=[bsrc[g], bvals[g]], writes=[bsrc[g]])
        for g in range(ngrp):
            sl = src[:, g, :]
            S.op("dve", lambda e, g=g, sl=sl: e.max(out=vals[:, g, 8:16], in_=sl), reads=[bsrc[g]], writes=[bvals[g]])
        for g in range(ngrp):
            sl = src[:, g, :]
            S.op("dve", lambda e, g=g, sl=sl: e.max_index(out=idx[:, g, 8:16], in_max=vals[:, g, 8:16], in_values=sl), reads=[bsrc[g], bvals[g]], writes=[bidx[g]])

    def phase_peer(self, l):
        S, P, I, pb = self.S, self.P, self.I, self.pb
        cfg = self.cfg
        self.issue_conv(None, upto_layer=l)
        self.phase()
        last = (l == DEPTH - 1)
        W, b_W = self.sb("W", [128, 3 * 128, 128], BF16)
        Rb = [self.sb("R", [128, 32, 128], BF16) for _ in range(2)]
        Cb = [self.sb("C", [128, 32, 128], BF16) for _ in range(2)]
        NU = 6
        ub = [self.sb("ub", [128, 8, 128], BF16) for _ in range(NU)]
        vb = [self.sb("vb", [128, D], BF16) for _ in range(NU)]
        XT, b_XT = self.sb("XT", [128, 3, 8, 128], BF16)
        ix, b_ix = self.sb("ix", [128, 3, 3, 128], BF16)
        g0b = [self.sb("g0", [128, 384], BF16) for _ in range(4)]
        g1b = [self.sb("g1", [128, 384], BF16) for _ in range(4)]
        x2t, b_x2t = self.sb("x2t", [128, D], F32)
        x3t, b_x3t = self.sb("x3t", [128, D], F32)
        junk, b_junk = self.sb("junk", [128, D], F32)
        st, b_st = self.sb("st", [128, 4], F32)
        yo, b_yo = self.sb("yo", [128, D], F32)
        slots = [(sg, s) for sg in self.segs() for s in self.mix_slots(sg, l)]
        groups = [slots[i:i + 3] for i in range(0, len(slots), 3)]
        iobf, b_iobf = self.sb("iobf", [128, 128], BF16)
        self.cp("dve", iobf[:], self.iota128[:], [self.b_iota128], [b_iobf])
        iob = iobf[:].unsqueeze(1).to_broadcast([128, 32, 128])
        qi = 0
        cc = 0
        for grp in groups:
            nt = len(grp)
            for ti, (sg, s) in enumerate(grp):
                sc = self.SC[sg]
                S.dma("sp", XT[:, ti], sc["xnT"][s], reads=[sc["b_xnT"][s]], writes=[b_XT])
                S.dma("sp", ix[:, ti], sc["ixT"][s], reads=[sc["b_ixT"][s]], writes=[b_ix])
            for ti in range(nt):
                for hf in range(4):
                    t0 = hf * 32
                    (R, b_R), (C, b_C) = Rb[qi % 2], Cb[qi % 2]
                    qi += 1
                    i1b = ix[:, ti, 0, t0:t0 + 32].unsqueeze(2).to_broadcast([128, 32, 128])
                    i2b = ix[:, ti, 1, t0:t0 + 32].unsqueeze(2).to_broadcast([128, 32, 128])
                    gbb = ix[:, ti, 2, t0:t0 + 32].unsqueeze(2).to_broadcast([128, 32, 128])
                    self.tt("dve", R[:], iob, i1b, ALU.is_equal, [b_iobf, b_ix], [b_R])
                    self.tt("pool", R[:], R[:], gbb, ALU.mult, [b_R, b_ix], [b_R])
                    self.tt("dve", C[:], iob, i2b, ALU.is_equal, [b_iobf, b_ix], [b_C])
                    for q4 in range(8):
                        bank = 6 + (q4 % 2)
                        for j in range(4):
                            t = q4 * 4 + j
                            self.mm(P[:, bank, j * 128:(j + 1) * 128], R[:, t, :], C[:, t, :], True, True, [b_R, b_C], [pb[bank]], inc=(j == 3))
                        tok0 = ti * 128 + t0 + q4 * 4
                        self.cp("act", W[:, tok0:tok0 + 4, :].rearrange("p t i -> p (t i)"), P[:, bank, :], [pb[bank]], [b_W])
            N = nt * 128
            XTf = XT[:].rearrange("p t k n -> p t (k n)")

            def load_uv(c, cc):
                u, bu = ub[cc % NU]
                v, bv = vb[cc % NU]
                S.dma("sp", u[:].rearrange("p k e -> p (k e)"), self.Ubf[l, c], reads=[self.b_Ubf[l][c]], writes=[bu])
                S.dma("sp", v[:], self.Vbf[l, c], reads=[self.b_Vbf[l][c]], writes=[bv])

            def issue_A(c, cc):
                u, bu = ub[cc % NU]
                bank = 6 + (cc % 2)
                for ti in range(nt):
                    for kc in range(8):
                        self.mm(P[:, bank, ti * 128:(ti + 1) * 128], u[:, kc, :], XT[:, ti, kc, :], kc == 0, kc == 7, [bu, b_XT], [pb[bank]],
                                inc=(kc == 7 and ti == nt - 1))

            PF = NU - 1
            for c in range(PF):
                load_uv(c, cc + c)

            def stage_A(c, cc):
                issue_A(c, cc)
                bank = 6 + (cc % 2)
                (g0, bg0), (g1, bg1) = g0b[cc % 4], g1b[cc % 4]
                self.act(g0[:, 0:N], P[:, bank, 0:N], AF.Gelu, [pb[bank]], [bg0])
                self.tt("dve", g1[:, 0:N], g0[:, 0:N], W[:, 0:N, c], ALU.mult, [bg0, b_W], [bg1])

            stage_A(0, cc)
            stage_A(1, cc + 1)
            for c in range(128):
                if c + PF < 128:
                    load_uv(c + PF, cc + PF)
                if c + 2 < 128:
                    stage_A(c + 2, cc + 2)
                v, bv = vb[cc % NU]
                g1, bg1 = g1b[cc % 4]
                for ti in range(nt):
                    for hf in range(2):
                        self.mm(P[:, 2 * ti + hf, :], g1[:, ti * 128:(ti + 1) * 128], v[:, hf * 512:(hf + 1) * 512], c == 0, c == 127,
                                [bg1, bv], [pb[2 * ti + hf]], inc=(hf == 1 and ti == nt - 1))
                cc += 1
            for ti, (sg, s) in enumerate(grp):
                sc = self.SC[sg]
                S.dma("sp", x2t[:], sc["x2"][s * 128:(s + 1) * 128, :], reads=[sc["b_x2"][s]], writes=[b_x2t])
                self.tt("dve", x3t[:].rearrange("p (a d) -> p a d", a=2), P[:, 2 * ti:2 * ti + 2, :], x2t[:].rearrange("p (a d) -> p a d", a=2), ALU.add,
                        [pb[2 * ti], pb[2 * ti + 1], b_x2t], [b_x3t])
                if not last:
                    S.dma("sp", sc["x3"][s * 128:(s + 1) * 128, :], x3t[:], reads=[b_x3t], writes=[sc["b_x3"][s]])
                else:
                    self.act(junk[:], x3t[:], AF.Square, [b_x3t], [b_junk, b_st], accum_out=st[:, 0:1])
                    self.ts("dve", st[:, 1:2], st[:, 0:1], 1.0 / D, 1e-6, ALU.mult, ALU.add, [b_st], [b_st])
                    self.act(st[:, 2:3], st[:, 1:2], AF.Ln, [b_st], [b_st])
                    self.act(st[:, 2:3], st[:, 2:3], AF.Exp, [b_st], [b_st], scale=-0.5)
                    self.stt("dve", yo[:], x3t[:], st[:, 2:3], self.gfin[:], ALU.mult, ALU.mult, [b_x3t, b_st, self.b_gfin], [b_yo])
                    if sg == "f":
                        dst, bd = self.O["yf"][0][s * 128:(s + 1) * 128, :], self.O["yf"][1][s]
                    else:
                        so = s - 2 * cfg.H
                        dst, bd = self.O["yp"][0][so * 128:(so + 1) * 128, :], self.O["yp"][1][so]
                    S.dma("sp", dst, yo[:], reads=[b_yo], writes=[bd])


def prep_shared(inp, cfg):
    f = lambda a: np.ascontiguousarray(np.asarray(a, dtype=np.float32))
    sh = {}
    w_in = f(inp["w_in"])
    swq = np.concatenate([np.arange(1920 + h * 64, 1920 + (h + 1) * 64) for h in (0, 3, 1, 4, 2, 5)])
    cols = np.concatenate([np.arange(0, 768), np.arange(1152, 1920), swq, np.arange(2304, 2432), np.arange(768, 1152), np.arange(2432, 2560)])
    sh["w_in"] = np.ascontiguousarray(w_in[:, :, cols])
    sh["w_out"] = f(inp["w_out"])
    for a, b in (("wxq", "w_xq"), ("wxk", "w_xk"), ("wxv", "w_xv"), ("wxo", "w_xo"), ("pwq", "peer_wq")):
        sh[a] = f(inp[b])
    sk = f(inp["peer_subkeys"])
    sh["subkT"] = np.ascontiguousarray(np.transpose(sk, (0, 4, 1, 2, 3)).reshape(DEPTH, 128, 16, 128))
    u = f(inp["peer_u"])
    v = f(inp["peer_v"])
    sh["u_arr"] = np.ascontiguousarray(u.reshape(DEPTH, 128, 128, 8, 128).transpose(0, 2, 4, 3, 1).reshape(DEPTH, 128, 128, D))
    sh["v_arr"] = np.ascontiguousarray(v.reshape(DEPTH, 128, 128, D).transpose(0, 2, 1, 3))
    g = np.stack([f(inp["norm_mix_g"]), f(inp["norm_xa_g"]), f(inp["norm_mem_g"]), f(inp["norm_ffn_g"])], axis=1)
    sh["gains"] = np.ascontiguousarray(np.broadcast_to(g[:, :, None, :], (DEPTH, 4, 128, D)))
    sh["gfinal"] = np.ascontiguousarray(np.broadcast_to(f(inp["final_g"])[None, :], (128, D)))
    cw = f(inp["conv_w"])
    sh["convw"] = np.ascontiguousarray(cw.reshape(DEPTH, 3, 2, 128).transpose(0, 3, 1, 2).reshape(DEPTH, 128, 6))
    sh["sink"] = np.ascontiguousarray(np.broadcast_to(f(inp["swa_sink"])[:, None, :], (DEPTH, 128, 6)))
    rpb = f(inp["na_rpb"])
    bias_off = f(inp["t5_bias"])[t5_bucket(np.arange(-128, 129))]
    sh["_rpb"] = rpb
    sh["_bias_off"] = bias_off
    nt = cfg.NT_F
    ji = nt // 2
    sh["na_int"] = np.stack([na_bias_block(rpb[l], nt, ji, list(range(ji - 2, ji + 3))).reshape(128, -1) for l in range(DEPTH)])
    sh["swa_int"] = np.stack([swa_bias_block(bias_off, nt, ji, [ji - 1, ji, ji + 1]).reshape(128, -1)] * DEPTH)
    spec = cfg.f_special()
    na_fs = np.zeros((DEPTH, len(spec), 128, 6 * 7 * 128), np.float32)
    for l in range(DEPTH):
        for i, j in enumerate(spec):
            k0, k1 = na_key_range(j, nt)
            blk = na_bias_block(rpb[l], nt, j, list(range(k0, k1 + 1))).reshape(128, -1)
            na_fs[l, i, :, :blk.shape[1]] = blk
    sh["na_fs"] = na_fs
    swa_fs = np.zeros((DEPTH, 2, 128, 6 * 3 * 128), np.float32)
    for i, j in enumerate(cfg.f_swa_special()):
        keys = list(range(max(j - 1, 0), min(j + 1, nt - 1) + 1))
        blk = swa_bias_block(bias_off, nt, j, keys).reshape(128, -1)
        swa_fs[:, i, :, :blk.shape[1]] = blk
    sh["swa_fs"] = swa_fs
    return sh


def prep_core(sh, cfg, x_full, mem_full, x_pseq=None, mem_p=None, q=0):
    m = {k: v for k, v in sh.items() if not k.startswith("_")}
    m["xf"] = np.ascontiguousarray(x_full, dtype=np.float32)
    m["memf"] = np.ascontiguousarray(mem_full, dtype=np.float32)
    if cfg.S_P:
        S_P, H = cfg.S_P, cfg.H
        g0 = q * cfg.NP_OUT - 2 * H
        xp = np.zeros((S_P * 128, D), np.float32)
        valid = np.zeros((128, S_P), np.float32)
        na_p = np.zeros((DEPTH, S_P, 128, 6 * 7 * 128), np.float32)
        swa_p = np.zeros((DEPTH, S_P, 128, 6 * 3 * 128), np.float32)
        for s in range(S_P):
            g = g0 + s
            if 0 <= g < cfg.NT_P:
                xp[s * 128:(s + 1) * 128] = x_pseq[g * 128:(g + 1) * 128]
                valid[:, s] = 1.0
            sw = swa_bias_block(sh["_bias_off"], cfg.NT_P, g, [g - 1, g, g + 1]).reshape(128, -1)
            for l in range(DEPTH):
                na_p[l, s] = na_bias_block(sh["_rpb"][l], cfg.NT_P, g, list(range(g - 3, g + 4))).reshape(128, -1)
                swa_p[l, s] = sw
        m["xp"] = xp
        m["validp"] = valid
        m["memp"] = np.ascontiguousarray(mem_p, dtype=np.float32)
        m["na_p"] = na_p
        m["swa_p"] = swa_p
    return m


_NC_CACHE = {}


def get_nc(cfg, dbg=False, stop=None, conv=True):
    key = (cfg.NT_F, cfg.NT_P, cfg.NP_OUT, cfg.H, dbg, stop, conv)
    if key not in _NC_CACHE:
        k = K(cfg, dbg, stop, conv)
        _NC_CACHE[key] = k.build()
    return _NC_CACHE[key]


def kernel(**inp):
    cfg = Cfg(64, 64, 16, 3)
    sh = prep_shared(inp, cfg)
    xs = np.asarray(inp["x_sample"], dtype=np.float32)
    xp = np.asarray(inp["x_prompt"], dtype=np.float32)
    ms = np.asarray(inp["mem_sample"], dtype=np.float32)
    mp = np.asarray(inp["mem_prompt"], dtype=np.float32)
    in_maps = []
    for c in range(8):
        p, q = c // 4, c % 4
        in_maps.append(prep_core(sh, cfg, xs[c], ms[c], xp[p], mp[p], q))
    nc = get_nc(cfg)
    res = run_bass_kernel_spmd(nc, in_maps, core_ids=list(range(8)))
    y_sample = np.stack([np.asarray(res.results[c]["yf"], dtype=np.float32).reshape(8192, D) for c in range(8)])
    y_prompt = np.zeros((2, 8192, D), np.float32)
    for c in range(8):
        p, q = c // 4, c % 4
        y_prompt[p, q * 2048:(q + 1) * 2048] = np.asarray(res.results[c]["yp"], dtype=np.float32)
    return (y_prompt, y_sample)
```

```python
import os
import numpy as np
import concourse.bass as bass
import concourse.mybir as mybir
from concourse.bass_utils import run_bass_kernel_spmd
from contextlib import ExitStack

F32 = mybir.dt.float32
BF16 = mybir.dt.bfloat16
U32 = mybir.dt.uint32
ALU = mybir.AluOpType
AF = mybir.ActivationFunctionType
AX = mybir.AxisListType

D = 1024
DEPTH = 2
NEG = -1e30
SB_BASE = 16512
SB_END = 229312


class Buf:
    __slots__ = ("name", "lw", "rd")

    def __init__(self, name):
        self.name = name
        self.lw = None
        self.rd = {}


class Sched:
    CE = ("pe", "act", "dve", "pool")
    ENG = ("pe", "act", "dve", "pool", "sp")
    EPOCH = 30000
    NDQ = 12

    def __init__(self, nc, es):
        self.nc = nc
        self.es = es
        self.sems = []
        self.prog = {e: [] for e in self.ENG}
        self.seen_c = {e: {} for e in self.ENG}
        self.seen_d = {e: {} for e in self.ENG}
        self.dq = {}
        for q in ("sp", "act", "pool"):
            self.dq[q] = dict(sems=[self._newsem("d_%s%d" % (q, i)) for i in range(self.NDQ)],
                              uses=[0] * self.NDQ, nxt=0)
        self.nops = 0
        self._cap = None

    def begin_capture(self):
        self._cap = []

    def end_capture(self):
        c = self._cap
        self._cap = None
        return c

    def feed(self, items):
        for it in items:
            if it[0] == "op":
                self.op(it[1], it[2], it[3], it[4])
            else:
                self.dma(it[1], it[2], it[3], it[4], it[5], it[6])

    def feed_merged(self, A, B):
        na, nb = len(A), len(B)
        if nb == 0:
            return self.feed(A)
        if na == 0:
            return self.feed(B)
        ia = ib = 0
        while ia < na or ib < nb:
            if ib >= nb or (ia < na and ia * nb <= ib * na):
                self.feed([A[ia]])
                ia += 1
            else:
                self.feed([B[ib]])
                ib += 1

    def _newsem(self, name):
        s = self.es.enter_context(self.nc.semaphore("%s_%d" % (name, len(self.sems))))
        self.sems.append(s)
        return len(self.sems) - 1

    def _deps(self, eng, reads, writes):
        need_c = {}
        need_d = {}

        def add(tok):
            if tok is None:
                return
            if tok[0] == "c":
                if need_c.get(tok[1], -1) < tok[2]:
                    need_c[tok[1]] = tok[2]
            else:
                if need_d.get(tok[1], 0) < tok[2]:
                    need_d[tok[1]] = tok[2]
        for b in reads:
            add(b.lw)
        for b in writes:
            add(b.lw)
            for t in b.rd.values():
                add(t)
        out = []
        sc, sd = self.seen_c[eng], self.seen_d[eng]
        for e2, i2 in need_c.items():
            if eng == "pe" and e2 == "pe":
                continue
            if sc.get(e2, -1) >= i2:
                continue
            sc[e2] = i2
            out.append(("c", e2, i2))
        for s, v in need_d.items():
            if sd.get(s, 0) >= v:
                continue
            sd[s] = v
            out.append(("d", s, v))
        return out

    def _update(self, tok, reads, writes):
        key = tok[1]
        for b in reads:
            old = b.rd.get(key)
            if old is None or old[2] < tok[2]:
                b.rd[key] = tok
        for b in writes:
            b.lw = tok
            b.rd = {}

    def op(self, eng, fn, reads=(), writes=(), inc=True):
        if self._cap is not None:
            self._cap.append(("op", eng, fn, tuple(reads), tuple(writes)))
            return
        waits = self._deps(eng, reads, writes)
        tok = ("c", eng, len(self.prog[eng]))
        self.prog[eng].append([waits, fn, "c", None])
        self._update(tok, reads, writes)
        self.nops += 1

    def dma(self, q, out, in_, reads=(), writes=(), slow=False):
        if self._cap is not None:
            self._cap.append(("dma", q, out, in_, tuple(reads), tuple(writes), slow))
            return
        ring = self.dq[q]
        k = ring["nxt"]
        ring["nxt"] = (k + 1) % self.NDQ
        waits = self._deps(q, reads, writes)
        s = ring["sems"][k]
        if ring["uses"][k] >= self.EPOCH // 16:
            s = ring["sems"][k] = self._newsem("d_" + q)
            ring["uses"][k] = 0
        if ring["uses"][k] > 0:
            v = 16 * ring["uses"][k]
            if self.seen_d[q].get(s, 0) < v:
                self.seen_d[q][s] = v
                waits.append(("d", s, v))
        ring["uses"][k] += 1
        tok = ("d", s, 16 * ring["uses"][k])
        if slow:
            fn = (lambda e, o=out, i=in_: e.dma_start(out=o, in_=i, allow_slow_non_contiguous=True))
        else:
            fn = (lambda e, o=out, i=in_: e.dma_start(out=o, in_=i))
        self.prog[q].append([waits, fn, "d", s])
        self._update(tok, reads, writes)
        self.nops += 1

    def wait_all(self, eng, bufs):
        waits = self._deps(eng, bufs, ())
        if waits:
            self.prog[eng].append([waits, None, "w", None])

    def barrier(self):
        toks = []
        for e in self.CE:
            last = None
            for i in range(len(self.prog[e]) - 1, -1, -1):
                if self.prog[e][i][2] == "c":
                    last = i
                    break
            if last is not None:
                toks.append(("c", e, last))
        for q in self.dq.values():
            for s, u in zip(q["sems"], q["uses"]):
                if u > 0:
                    toks.append(("d", s, 16 * u))
        for e in self.ENG:
            waits = []
            for t in toks:
                if t[0] == "c":
                    if self.seen_c[e].get(t[1], -1) >= t[2]:
                        continue
                    self.seen_c[e][t[1]] = t[2]
                else:
                    if self.seen_d[e].get(t[1], 0) >= t[2]:
                        continue
                    self.seen_d[e][t[1]] = t[2]
                waits.append(t)
            if waits:
                self.prog[e].append([waits, None, "w", None])

    def emit(self):
        nc = self.nc
        engobj = dict(pe="tensor", act="scalar", dve="vector", pool="gpsimd", sp="sync")
        need = {e: set() for e in self.ENG}
        for e in self.ENG:
            for waits, fn, kind, ds in self.prog[e]:
                for t in waits:
                    if t[0] == "c":
                        need[t[1]].add(t[2])
        sig = {}
        for e in self.CE:
            cur = self._newsem("c_" + e)
            cnt = 0
            for i in sorted(need[e]):
                if cnt >= self.EPOCH:
                    cur = self._newsem("c_" + e)
                    cnt = 0
                cnt += 1
                sig[(e, i)] = (cur, cnt)
        self.nsig = len(sig)
        sems = self.sems

        def replay(name):
            def f(e):
                for i, (waits, fn, kind, ds) in enumerate(self.prog[name]):
                    for t in waits:
                        if t[0] == "c":
                            s, v = sig[(t[1], t[2])]
                        else:
                            s, v = t[1], t[2]
                        e.wait_ge(sems[s], v)
                    if fn is None:
                        continue
                    ins = fn(e)
                    if kind == "d":
                        ins.then_inc(sems[ds], 16)
                    elif (name, i) in sig:
                        ins.then_inc(sems[sig[(name, i)][0]], 1)
            return f
        with nc.Block() as block:
            for name in self.ENG:
                getattr(block, engobj[name])(replay(name))


def t5_bucket(rel):
    nb = 16
    max_exact = 8
    ret = (rel > 0).astype(np.int32) * nb
    n = np.abs(rel)
    large = max_exact + (np.log(np.maximum(n, 1) / max_exact) / np.log(128 / max_exact) * (nb - max_exact)).astype(np.int32)
    large = np.minimum(large, nb - 1)
    return (ret + np.where(n < max_exact, n, large)).astype(np.int32)


def na_row0(r, rows):
    return int(np.clip(r - 4, 0, rows - 8))


def na_key_range(j, nt):
    rows = 2 * nt
    lo = na_row0(2 * j, rows)
    hi = na_row0(2 * j + 1, rows) + 7
    return lo // 2, hi // 2


def na_bias_block(rpb, nt, j, keys):
    nk = len(keys)
    out = np.zeros((128, 6, nk, 128), np.float32)
    if j < 0 or j >= nt:
        return out
    rows = 2 * nt
    qr = np.arange(128) // 64
    qc = np.arange(128) % 64
    r = 2 * j + qr
    r0 = np.clip(r - 4, 0, rows - 8)
    cs = np.clip(qc - 8, 0, 48)
    kr = np.arange(128) // 64
    kc = np.arange(128) % 64
    for ki, kt in enumerate(keys):
        if kt < 0 or kt >= nt:
            out[:, :, ki, :] = NEG
            continue
        R = 2 * kt + kr
        okr = (R[:, None] >= r0[None, :]) & (R[:, None] < r0[None, :] + 8)
        okc = (kc[:, None] >= cs[None, :]) & (kc[:, None] < cs[None, :] + 16)
        ok = okr & okc
        dr = np.clip(R[:, None] - r[None, :] + 7, 0, 14)
        dc = np.clip(kc[:, None] - qc[None, :] + 15, 0, 30)
        b = rpb[dr, dc]
        b = np.where(ok[:, :, None], b, np.float32(NEG))
        out[:, :, ki, :] = np.transpose(b, (0, 2, 1))
    return out


def swa_bias_block(bias_off, nt, j, keys):
    nk = len(keys)
    out = np.zeros((128, 6, nk, 128), np.float32)
    if j < 0 or j >= nt:
        return out
    a = np.arange(128)
    for ki, kt in enumerate(keys):
        if kt < 0 or kt >= nt:
            out[:, :, ki, :] = NEG
            continue
        off = (kt - j) * 128 + a[:, None] - a[None, :]
        ok = np.abs(off) <= 128
        b = bias_off[np.clip(off + 128, 0, 256)]
        b = np.where(ok[:, :, None], b, np.float32(NEG))
        out[:, :, ki, :] = np.transpose(b, (0, 2, 1))
    return out


class Cfg:
    def __init__(self, nt_f=64, nt_p=64, np_out=16, halo=3):
        self.NT_F = nt_f
        self.NT_P = nt_p
        self.NP_OUT = np_out
        self.H = halo
        self.S_P = np_out + 4 * halo if np_out > 0 else 0

    def f_special(self):
        nt = self.NT_F
        return [j for j in range(nt) if not (2 <= j <= nt - 3)]

    def f_swa_special(self):
        return [0, self.NT_F - 1]


class K:
    def __init__(self, cfg, dbg=False, stop=None, conv=True):
        self.cfg = cfg
        self.dbg = dbg
        self.stop = stop
        self.conv = conv
        self.nc = bass.Bass("TRN2", target_bir_lowering=False)
        self.uid = 0

    def sb(self, name, shape, dt):
        esz = 4 if dt in (F32, U32) else 2
        n = int(np.prod(shape[1:])) * esz
        n = (n + 31) // 32 * 32
        assert self.off + n <= SB_END, ("SBUF overflow", name, self.off, n)
        self.uid += 1
        t = self.nc.alloc_sbuf_tensor_at("%s_%d" % (name, self.uid), list(shape), dt, offset=self.off)
        self.off += n
        return t, Buf(name)

    def din(self, name, shape, dt=F32):
        return self.nc.dram_tensor(name, list(shape), dt, kind="ExternalInput").ap(), Buf(name)

    def dscr(self, name, shape, dt, out=False):
        kind = "ExternalOutput" if (out or self.dbg) else "Internal"
        return self.nc.dram_tensor(name, list(shape), dt, kind=kind).ap()

    def mm(self, out, lhsT, rhs, start, stop, r, w, inc=True):
        self.S.op("pe", lambda e: e.matmul(out, lhsT=lhsT, rhs=rhs, start=start, stop=stop), reads=r, writes=w, inc=inc)

    def tr(self, out, in_, r, w, inc=True):
        ident = self.ident
        self.S.op("pe", lambda e: e.transpose(out=out, in_=in_, identity=ident[:]), reads=list(r) + [self.b_ident], writes=w, inc=inc)

    def act(self, out, in_, func, r, w, **kw):
        self.S.op("act", lambda e: e.activation(out=out, in_=in_, func=func, **kw), reads=r, writes=w)

    def tt(self, eng, out, in0, in1, op, r, w):
        self.S.op(eng, lambda e: e.tensor_tensor(out=out, in0=in0, in1=in1, op=op), reads=r, writes=w)

    def ts(self, eng, out, in0, s1, s2, op0, op1, r, w):
        if s2 is None:
            self.S.op(eng, lambda e: e.tensor_scalar(out=out, in0=in0, scalar1=s1, scalar2=None, op0=op0), reads=r, writes=w)
        else:
            self.S.op(eng, lambda e: e.tensor_scalar(out=out, in0=in0, scalar1=s1, scalar2=s2, op0=op0, op1=op1), reads=r, writes=w)

    def stt(self, eng, out, in0, scalar, in1, op0, op1, r, w):
        self.S.op(eng, lambda e: e.scalar_tensor_tensor(out=out, in0=in0, scalar=scalar, in1=in1, op0=op0, op1=op1), reads=r, writes=w)

    def cp(self, eng, out, in_, r, w):
        if eng == "act":
            self.S.op("act", lambda e: e.copy(out=out, in_=in_), reads=r, writes=w)
        else:
            self.S.op(eng, lambda e: e.tensor_copy(out=out, in_=in_), reads=r, writes=w)

    def memset(self, eng, ap, val, w):
        self.S.op(eng, lambda e: e.memset(ap, val), writes=w)

    def issue_conv(self, n=None, upto_layer=None):
        q = getattr(self, "conv_q", None)
        if not q:
            return
        k = 0
        while q and (n is None or k < n):
            l2, c, t = q[0]
            if upto_layer is not None and l2 > upto_layer:
                break
            q.pop(0)
            k += 1
            if t == "u":
                self.S.dma("pool", self.Ubf[l2, c], self.I["u_arr"][0][l2, c], writes=[self.b_Ubf[l2][c]])
            else:
                self.S.dma("pool", self.Vbf[l2, c], self.I["v_arr"][0][l2, c], writes=[self.b_Vbf[l2][c]])

    def phase(self):
        self.S.barrier()
        self.off = self.off_global

    def rms(self, x, bx, gb, bgb, out, bout, junk, bjunk, st, bst, valid=None, bvalid=None):
        self.act(junk[:], x[:], AF.Square, [bx], [bjunk, bst], accum_out=st[:, 0:1])
        self.ts("dve", st[:, 1:2], st[:, 0:1], 1.0 / D, 1e-6, ALU.mult, ALU.add, [bst], [bst])
        self.act(st[:, 2:3], st[:, 1:2], AF.Ln, [bst], [bst])
        self.act(st[:, 2:3], st[:, 2:3], AF.Exp, [bst], [bst], scale=-0.5)
        if valid is not None:
            self.tt("dve", st[:, 2:3], st[:, 2:3], valid, ALU.mult, [bst, bvalid], [bst])
        self.stt("dve", out[:], x[:], st[:, 2:3], gb[:], ALU.mult, ALU.mult, [bx, bst, bgb], [bout])

    def transpose_to(self, src, bsrc, n, bank, dst, bdst, evac="act"):
        P = self.P
        psb = P[:, bank, :].bitcast(BF16)
        for c in range(n):
            self.tr(psb[:, c * 128:(c + 1) * 128], src[:, c * 128:(c + 1) * 128], [bsrc], [self.pb[bank]], inc=(c == n - 1))
        self.cp(evac, dst, psb[:, 0:n * 128], [self.pb[bank]], [bdst])

    def build(self):
        cfg = self.cfg
        nc = self.nc
        NT_F, S_P, H = cfg.NT_F, cfg.S_P, cfg.H
        segs = [("f", NT_F)] + ([("p", S_P)] if S_P else [])
        self.es = ExitStack()
        es = self.es
        S = self.S = Sched(nc, es)
        self.off = SB_BASE

        I = {}
        I["xf"] = self.din("xf", [NT_F * 128, D])
        I["memf"] = self.din("memf", [256, D])
        if S_P:
            I["xp"] = self.din("xp", [S_P * 128, D])
            I["memp"] = self.din("memp", [256, D])
            I["validp"] = self.din("validp", [128, S_P])
        I["gains"] = self.din("gains", [DEPTH, 4, 128, D])
        I["gfinal"] = self.din("gfinal", [128, D])
        I["w_in"] = self.din("w_in", [DEPTH, D, 2560])
        I["w_out"] = self.din("w_out", [DEPTH, D, D])
        I["wxq"] = self.din("wxq", [DEPTH, D, 512])
        I["wxk"] = self.din("wxk", [DEPTH, D, 512])
        I["wxv"] = self.din("wxv", [DEPTH, D, 512])
        I["wxo"] = self.din("wxo", [DEPTH, 512, D])
        I["pwq"] = self.din("pwq", [DEPTH, D, 2048])
        I["subkT"] = self.din("subkT", [DEPTH, 128, 16, 128])
        I["u_arr"] = self.din("u_arr", [DEPTH, 128, 128, D])
        I["v_arr"] = self.din("v_arr", [DEPTH, 128, 128, D])
        I["convw"] = self.din("convw", [DEPTH, 128, 6])
        I["sink"] = self.din("sink", [DEPTH, 128, 6])
        I["na_int"] = self.din("na_int", [DEPTH, 128, 6 * 5 * 128])
        I["swa_int"] = self.din("swa_int", [DEPTH, 128, 6 * 3 * 128])
        nfs = len(cfg.f_special())
        I["na_fs"] = self.din("na_fs", [DEPTH, nfs, 128, 6 * 7 * 128])
        I["swa_fs"] = self.din("swa_fs", [DEPTH, 2, 128, 6 * 3 * 128])
        if S_P:
            I["na_p"] = self.din("na_p", [DEPTH, S_P, 128, 6 * 7 * 128])
            I["swa_p"] = self.din("swa_p", [DEPTH, S_P, 128, 6 * 3 * 128])
        self.I = I

        O = {}
        O["yf"] = (nc.dram_tensor("yf", [NT_F * 128, D], F32, kind="ExternalOutput").ap(), [Buf("yf") for _ in range(NT_F)])
        if S_P:
            O["yp"] = (nc.dram_tensor("yp", [cfg.NP_OUT * 128, D], F32, kind="ExternalOutput").ap(), [Buf("yp") for _ in range(cfg.NP_OUT)])
        self.O = O

        SC = {}
        for sg, ns in segs:
            SC[sg] = dict(
                zT=self.dscr("zT" + sg, [16, 128, ns * 128], BF16), b_zT=[Buf("zT") for _ in range(ns)],
                Vs=self.dscr("Vs" + sg, [ns, 128, 640], BF16), b_Vs=[Buf("Vs") for _ in range(ns)],
                x1=self.dscr("x1" + sg, [ns * 128, D], F32), b_x1=[Buf("x1") for _ in range(ns)],
                x2=self.dscr("x2" + sg, [ns * 128, D], F32), b_x2=[Buf("x2") for _ in range(ns)],
                x3=self.dscr("x3" + sg, [ns * 128, D], F32), b_x3=[Buf("x3") for _ in range(ns)],
                xnT=self.dscr("xnT" + sg, [ns, 128, 8, 128], BF16), b_xnT=[Buf("xnT") for _ in range(ns)],
                ixT=self.dscr("ixT" + sg, [ns, 128, 3, 128], BF16), b_ixT=[Buf("ixT") for _ in range(ns)],
                b_zpad=Buf("zpad"),
            )
        self.SC = SC
        self.Ubf = nc.dram_tensor("Ubf", [DEPTH, 128, 128, D], BF16, kind="Internal").ap()
        self.Vbf = nc.dram_tensor("Vbf", [DEPTH, 128, 128, D], BF16, kind="Internal").ap()
        self.b_Ubf = [[Buf("Ubf") for _ in range(128)] for l in range(DEPTH)]
        self.b_Vbf = [[Buf("Vbf") for _ in range(128)] for l in range(DEPTH)]

        self.P = nc.alloc_psum_tensor("P", [128, 8, 512], F32)
        self.pb = [Buf("pb%d" % i) for i in range(8)]
        self.ident, self.b_ident = self.sb("ident", [128, 128], BF16)
        identf, b_identf = self.sb("identf", [128, 128], F32)
        self.iota16, self.b_iota16 = self.sb("iota16", [128, 16], F32)
        self.iota128, self.b_iota128 = self.sb("iota128", [128, 128], F32)
        self.gb, self.b_gb = self.sb("gb", [128, 4, D], F32)
        self.gfin, self.b_gfin = self.sb("gfin", [128, D], F32)
        self.zero, self.b_zero = self.sb("zero", [128, 16], BF16)
        if S_P:
            self.validp, self.b_validp = self.sb("validp", [128, S_P], F32)
        self.off_global = self.off

        S.op("pool", lambda e: e.iota(identf[:], pattern=[[1, 128]], base=0, channel_multiplier=-1, allow_small_or_imprecise_dtypes=True), writes=[b_identf])
        self.ts("dve", self.ident[:], identf[:], 0.0, None, ALU.is_equal, None, [b_identf], [self.b_ident])
        io16 = self.iota16
        io128 = self.iota128
        S.op("pool", lambda e: e.iota(io16[:], pattern=[[1, 16]], base=0, channel_multiplier=0, allow_small_or_imprecise_dtypes=True), writes=[self.b_iota16])
        S.op("pool", lambda e: e.iota(io128[:], pattern=[[1, 128]], base=0, channel_multiplier=0, allow_small_or_imprecise_dtypes=True), writes=[self.b_iota128])
        self.memset("dve", self.zero[:], 0.0, [self.b_zero])
        S.dma("sp", self.gfin[:], I["gfinal"][0], writes=[self.b_gfin])
        if S_P:
            S.dma("sp", self.validp[:], I["validp"][0], writes=[self.b_validp])
        for l in range(DEPTH):
            S.dma("sp", self.gb[:], I["gains"][0][l].rearrange("g p d -> p g d"), writes=[self.b_gb])
            self.phase_z(l)
            if self.stop == "z%d" % l:
                break
            if l == 0 and self.conv:
                self.conv_q = [(l2, c, t) for l2 in range(DEPTH) for c in range(128) for t in ("u", "v")]
            self.phase_mix(l)
            if self.stop == "m%d" % l:
                break
            self.phase_xa(l)
            if self.stop == "x%d" % l:
                break
            self.phase_peer(l)
            if self.stop == "p%d" % l:
                break
        outs = list(O["yf"][1]) + (list(O["yp"][1]) if S_P else [])
        S.wait_all("sp", outs)
        S.barrier()
        S.emit()
        return nc

    def z_slots(self, sg, l):
        cfg = self.cfg
        if sg == "f":
            return list(range(cfg.NT_F))
        return list(range(cfg.S_P)) if l == 0 else list(range(cfg.H, cfg.S_P - cfg.H))

    def mix_slots(self, sg, l):
        cfg = self.cfg
        if sg == "f":
            return list(range(cfg.NT_F))
        return list(range(cfg.H, cfg.S_P - cfg.H)) if l == 0 else list(range(2 * cfg.H, cfg.S_P - 2 * cfg.H))

    def segs(self):
        return ["f"] + (["p"] if self.cfg.S_P else [])

    def x_in(self, sg, l):
        if l == 0:
            return self.I["xf" if sg == "f" else "xp"][0], None
        return self.SC[sg]["x3"], self.SC[sg]["b_x3"]

    def phase_z(self, l):
        S, P, I = self.S, self.P, self.I
        self.phase()
        win, b_win = self.sb("win", [128, 8, 2560], BF16)
        for kc in range(8):
            S.dma("pool", win[:, kc, :], I["w_in"][0][l, kc * 128:(kc + 1) * 128, :], writes=[b_win])
        NB = 3
        xt = [self.sb("xt", [128, D], F32) for _ in range(NB)]
        junks = [self.sb("junk", [128, D], F32) for _ in range(NB)]
        st = [self.sb("st", [128, 4], F32) for _ in range(NB)]
        hb = [self.sb("hb", [128, D], BF16) for _ in range(NB)]
        hT = [self.sb("hT", [128, 8, 128], BF16) for _ in range(NB)]
        zs = [self.sb("zs", [128, 16, 128], BF16) for _ in range(NB)]
        vs = [self.sb("vs", [128, 8, 80], BF16) for _ in range(NB)]
        for i in range(NB):
            self.memset("dve", vs[i][0][:], 0.0, [vs[i][1]])
            self.memset("dve", vs[i][0][:, :, 64:65], 1.0, [vs[i][1]])
        it = 0
        BIS = int(os.environ.get("BIS", "99"))
        if BIS == 0:
            return
        zlist = [(sg, s) for sg in self.segs() for s in self.z_slots(sg, l)]
        for sg in self.segs():
            sc = self.SC[sg]
            xin, bxin = self.x_in(sg, l)
            for s in self.z_slots(sg, l):
                b = it % NB
                par = it % 2
                junk, b_junk = junks[b]
                x, bx = xt[b]
                if it == 0:
                    S.dma("sp", x[:], xin[s * 128:(s + 1) * 128, :], reads=([bxin[s]] if bxin else []), writes=[bx])
                if it + 1 < len(zlist):
                    sg2, s2 = zlist[it + 1]
                    xin2, bxin2 = self.x_in(sg2, l)
                    x2_, bx2_ = xt[(it + 1) % NB]
                    S.dma("sp", x2_[:], xin2[s2 * 128:(s2 + 1) * 128, :], reads=([bxin2[s2]] if bxin2 else []), writes=[bx2_])
                it += 1
                h, bh = hb[b]
                if sg == "p":
                    self.rms(x, bx, self.gb[:, 0, :], self.b_gb, h, bh, junk, b_junk, st[b][0], st[b][1],
                             valid=self.validp[:, s:s + 1], bvalid=self.b_validp)
                else:
                    self.rms(x, bx, self.gb[:, 0, :], self.b_gb, h, bh, junk, b_junk, st[b][0], st[b][1])
                if BIS == 1:
                    continue
                t, bt = hT[b]
                self.transpose_to(h, bh, 8, 4 * par, t[:].rearrange("p c t -> p (c t)"), bt)
                if BIS == 2:
                    continue
                z, bz = zs[b]
                for g in range(4):
                    bank = 4 * par + 1 + (g % 2)
                    for j in range(4):
                        c = g * 4 + j
                        for kc in range(8):
                            self.mm(P[:, bank, j * 128:(j + 1) * 128], win[:, kc, c * 128:(c + 1) * 128], t[:, kc, :],
                                    kc == 0, kc == 7, [b_win, bt], [self.pb[bank]], inc=(j == 3 and kc == 7))
                    self.cp("act" if g % 2 == 0 else "dve", z[:, g * 4:(g + 1) * 4, :].rearrange("p c t -> p (c t)"), P[:, bank, :], [self.pb[bank]], [bz])
                if BIS == 3:
                    continue
                v, bv = vs[b]
                vbk = 4 * par + 3
                for kc in range(8):
                    self.mm(P[:, vbk, :], t[:, kc, :], win[:, kc, 2048:2560], kc == 0, kc == 7, [b_win, bt], [self.pb[vbk]], inc=(kc == 7))
                self.cp("dve", v[:, :, 0:64], P[:, vbk, :].rearrange("p (h d) -> p h d", d=64), [self.pb[vbk]], [bv])
                S.dma("sp", sc["zT"][:, :, s * 128:(s + 1) * 128].rearrange("c p t -> p c t"), z[:], reads=[bz], writes=[sc["b_zT"][s]])
                S.dma("sp", sc["Vs"][s], v[:].rearrange("p h d -> p (h d)"), reads=[bv], writes=[sc["b_Vs"][s]])

    def na_keys(self, sg, s):
        cfg = self.cfg
        if sg == "f":
            k0, k1 = na_key_range(s, cfg.NT_F)
            spec = cfg.f_special()
            return k0, k1, (spec.index(s) if s in spec else None)
        return s - 3, s + 3, s

    def swa_keys(self, sg, s):
        cfg = self.cfg
        if sg == "f":
            k0, k1 = max(s - 1, 0), min(s + 1, cfg.NT_F - 1)
            spec = cfg.f_swa_special()
            return k0, k1, (spec.index(s) if s in spec else None)
        return s - 1, s + 1, s

    def _mix_loads(self, S, I, l, sg, s, sc, zT, Vs, xin, bxin, k0, k1, nk, w0, w1, nw, nsp, ssp, c0, zdeps, vdeps,
                   q_n, bqn, q_s, bqs, sc_, bsc, k_n, bkn, v_n, bvn, k_s, bks, v_w, bvw, x, bx, qn4, nasp, b_nasp, swsp, b_swsp):
        S.dma("sp", qn4[0:64, :, 0, :], zT[0:3, 0:64, c0:c0 + 128].rearrange("c p t -> p c t"), reads=zdeps, writes=[bqn])
        S.dma("sp", qn4[64:128, :, 1, :], zT[0:3, 64:128, c0:c0 + 128].rearrange("c p t -> p c t"), reads=zdeps, writes=[bqn])
        S.dma("sp", k_n[:, :, 0:nk * 128], zT[3:6, :, k0 * 128:(k1 + 1) * 128].rearrange("c p t -> p c t"), reads=zdeps, writes=[bkn])
        S.dma("sp", v_n[:, 0:nk, :], Vs[k0:k1 + 1, :, 0:480].rearrange("k p f -> p k f"), reads=vdeps, writes=[bvn])
        if nsp is not None:
            src = I["na_fs"][0][l, nsp] if sg == "f" else I["na_p"][0][l, s]
            S.dma("sp", nasp[:, 0:6 * nk * 128], src[:, 0:6 * nk * 128], writes=[b_nasp])
        S.dma("sp", q_s[0:64, 0:3, :], zT[12:15, 0:64, c0:c0 + 128].rearrange("c p t -> p c t"), reads=zdeps, writes=[bqs])
        S.dma("sp", q_s[64:128, 3:6, :], zT[12:15, 64:128, c0:c0 + 128].rearrange("c p t -> p c t"), reads=zdeps, writes=[bqs])
        S.dma("sp", k_s[:, 0:nw * 128], zT[15, :, w0 * 128:(w1 + 1) * 128], reads=zdeps, writes=[bks])
        S.dma("sp", v_w[:, 0:nw, :], Vs[w0:w1 + 1, :, 480:640].rearrange("k p f -> p k f"), reads=vdeps, writes=[bvw])
        if ssp is not None:
            src = I["swa_fs"][0][l, ssp] if sg == "f" else I["swa_p"][0][l, s]
            S.dma("sp", swsp[:, 0:6 * nw * 128], src[:, 0:6 * nw * 128], writes=[b_swsp])
        lo = 1 if (sg == "f" and s == 0) else 0
        hi = 1 if (sg == "f" and s == self.cfg.NT_F - 1) else 0
        S.dma("sp", sc_[:, :, lo:130 - hi], zT[6:12, :, c0 - 1 + lo:c0 + 129 - hi].rearrange("c p t -> p c t"), reads=zdeps, writes=[bsc])
        if lo:
            self.memset("dve", sc_[:, :, 0:1], 0.0, [bsc])
        if hi:
            self.memset("dve", sc_[:, :, 129:130], 0.0, [bsc])
        S.dma("sp", x[:], xin[s * 128:(s + 1) * 128, :], reads=([bxin[s]] if bxin else []), writes=[bx])


    def phase_mix(self, l):
        S, P, I, pb = self.S, self.P, self.I, self.pb
        self.phase()
        wout, b_wout = self.sb("wout", [128, 8, D], BF16)
        for kc in range(8):
            S.dma("pool", wout[:, kc, :], I["w_out"][0][l, kc * 128:(kc + 1) * 128, :], writes=[b_wout])
        naint, b_naint = self.sb("naint", [128, 6 * 5 * 128], F32)
        swint, b_swint = self.sb("swint", [128, 6 * 3 * 128], F32)
        S.dma("sp", naint[:], I["na_int"][0][l], writes=[b_naint])
        S.dma("sp", swint[:], I["swa_int"][0][l], writes=[b_swint])
        naspb = [self.sb("nasp", [128, 6 * 7 * 128], F32) for _ in range(2)]
        swspb = [self.sb("swsp", [128, 6 * 3 * 128], F32) for _ in range(2)]
        cw, b_cw = self.sb("cw", [128, 6], F32)
        S.dma("sp", cw[:], I["convw"][0][l], writes=[b_cw])
        es_, b_es = self.sb("es", [128, 6], F32)
        S.dma("sp", es_[:], I["sink"][0][l], writes=[b_es])
        self.act(es_[:], es_[:], AF.Exp, [b_es], [b_es])
        NB = 3
        qn = [self.sb("qn", [128, 6, 128], BF16) for _ in range(NB)]
        qs = [self.sb("qs", [128, 6, 128], BF16) for _ in range(NB)]
        for i in range(NB):
            self.memset("dve", qn[i][0][:], 0.0, [qn[i][1]])
            self.memset("dve", qs[i][0][:], 0.0, [qs[i][1]])
        scb = [self.sb("scb", [128, 6, 130], BF16) for _ in range(NB)]
        kn = [self.sb("kn", [128, 3, 7 * 128], BF16) for _ in range(NB)]
        vn = [self.sb("vn", [128, 7, 480], BF16) for _ in range(NB)]
        ks = [self.sb("ks", [128, 3 * 128], BF16) for _ in range(NB)]
        vw = [self.sb("vw", [128, 3, 160], BF16) for _ in range(NB)]
        xt = [self.sb("xt", [128, D], F32) for _ in range(NB)]
        sgb = [self.sb("sg", [128, 512], F32) for _ in range(2)]
        ptb = [self.sb("pt", [128, 512], BF16) for _ in range(2)]
        yb = [self.sb("y", [128, 768], BF16)] * 2
        yTb = [self.sb("yT", [128, 6, 128], BF16)] * 2
        yscb = [self.sb("ysc", [128, 2, 128], BF16)] * 2
        ub_ = [self.sb("u", [128, 2, 130], F32)] * 2
        t1b = [self.sb("t1", [128, 2, 128], F32)] * 2
        t2b = [self.sb("t2", [128, 2, 128], F32)] * 2
        rdb = [self.sb("rd", [128, 12], F32)] * 2
        x1b = [self.sb("x1", [128, D], F32)] * 2
        gi = [0]

        def attn(nk, kfn, qfn, vfn, bias, b_bias, po, rbufs):
            pairs = [(h, ki) for h in range(6) for ki in range(nk)]
            for g0 in range(0, len(pairs), 4):
                grp = pairs[g0:g0 + 4]
                n = len(grp)
                k = gi[0] % 2
                k3 = gi[0] % 2
                gi[0] += 1
                bank = k
                for j, (h, ki) in enumerate(grp):
                    self.mm(P[:, bank, j * 128:(j + 1) * 128], kfn(h, ki), qfn(h), True, True, rbufs, [pb[bank]], inc=(j == n - 1))
                sg_, bsg = sgb[k3]
                pt, bpt = ptb[k3]
                self.stt("dve", sg_[:, 0:n * 128], P[:, bank, 0:n * 128], 0.125, bias[:, g0 * 128:(g0 + n) * 128], ALU.mult, ALU.add, [pb[bank], b_bias], [bsg])
                self.act(pt[:, 0:n * 128], sg_[:, 0:n * 128], AF.Exp, [bsg], [bpt])
                for j, (h, ki) in enumerate(grp):
                    self.mm(P[:, po, h * 80:h * 80 + 65], pt[:, j * 128:(j + 1) * 128], vfn(h, ki), ki == 0, ki == nk - 1, [bpt] + rbufs, [pb[po]], inc=(j == n - 1))

        spb = {}

        def front(sg, s, b, b3, do_loads, do_compute):
            sc = self.SC[sg]
            zT, Vs = sc["zT"], sc["Vs"]
            xin, bxin = self.x_in(sg, l)
            k0, k1, nsp = self.na_keys(sg, s)
            nk = k1 - k0 + 1
            w0, w1, ssp = self.swa_keys(sg, s)
            nw = w1 - w0 + 1
            c0 = s * 128
            zdeps = [sc["b_zT"][k] for k in range(k0, k1 + 1)]
            vdeps = [sc["b_Vs"][k] for k in range(k0, k1 + 1)]
            (q_n, bqn), (q_s, bqs), (sc_, bsc) = qn[b3], qs[b3], scb[b3]
            (k_n, bkn), (v_n, bvn), (k_s, bks), (v_w, bvw) = kn[b3], vn[b3], ks[b3], vw[b3]
            x, bx = xt[b3]
            qn4 = q_n[:].rearrange("p (c two) t -> p c two t", two=2)
            (nasp, b_nasp), (swsp, b_swsp) = naspb[b], swspb[b]
            if nsp is None:
                nab, bnab = naint, b_naint
            else:
                nab, bnab = nasp, b_nasp
            if ssp is None:
                swb, bswb = swint, b_swint
            else:
                swb, bswb = swsp, b_swsp
            if do_loads:
                self._mix_loads(S, I, l, sg, s, sc, zT, Vs, xin, bxin, k0, k1, nk, w0, w1, nw, nsp, ssp, c0, zdeps, vdeps,
                                q_n, bqn, q_s, bqs, sc_, bsc, k_n, bkn, v_n, bvn, k_s, bks, v_w, bvw, x, bx, qn4, nasp, b_nasp, swsp, b_swsp)
            if not do_compute:
                return
            po = 2 + 2 * b
            attn(nk,
                 lambda h, ki: k_n[:, h // 2, ki * 128:(ki + 1) * 128],
                 lambda h: q_n[:, h, :],
                 lambda h, ki: v_n[:, ki, h * 80:h * 80 + 65],
                 nab, bnab, po, [bkn, bqn, bvn])
            attn(nw,
                 lambda h, ki: k_s[:, ki * 128:(ki + 1) * 128],
                 lambda h: q_s[:, h, :],
                 lambda h, ki: v_w[:, ki, (h // 3) * 80:(h // 3) * 80 + 65],
                 swb, bswb, po + 1, [bks, bqs, bvw])

        def tail(sg, s, b, b3):
            sc = self.SC[sg]
            (sc_, bsc) = scb[b3]
            x, bx = xt[b3]
            (y, b_y), (yT, b_yT), (ysc, b_ysc) = yb[b], yTb[b], yscb[b]
            (u_, b_u), (t1, b_t1), (t2, b_t2), (rd, b_rd), (x1, b_x1t) = ub_[b], t1b[b], t2b[b], rdb[b], x1b[b]
            po = 2 + 2 * b
            pna = P[:, po, 0:480].rearrange("p (h d) -> p h d", d=80)
            psw = P[:, po + 1, 0:480].rearrange("p (h d) -> p h d", d=80)
            self.cp("dve", rd[:, 0:6], pna[:, :, 64], [pb[po]], [b_rd])
            self.tt("dve", rd[:, 6:12], psw[:, :, 64], es_[:], ALU.add, [pb[po + 1], b_es], [b_rd])
            S.op("dve", lambda e: e.reciprocal(out=rd[:], in_=rd[:]), reads=[b_rd], writes=[b_rd])
            self.tt("dve", y[:, 0:384].rearrange("p (h d) -> p h d", d=64), pna[:, :, 0:64], rd[:, 0:6].unsqueeze(2).to_broadcast([128, 6, 64]), ALU.mult, [pb[po], b_rd], [b_y])
            self.tt("dve", y[:, 384:768].rearrange("p (h d) -> p h d", d=64), psw[:, :, 0:64], rd[:, 6:12].unsqueeze(2).to_broadcast([128, 6, 64]), ALU.mult, [pb[po + 1], b_rd], [b_y])
            self.transpose_to(y, b_y, 6, 6, yT[:].rearrange("p c t -> p (c t)"), b_yT, evac="act")
            self.tt("dve", u_[:], sc_[:, 2:4, :], sc_[:, 4:6, :], ALU.mult, [bsc], [b_u])
            for c in range(2):
                self.ts("dve", t1[:, c, :], u_[:, c, 0:128], cw[:, c:c + 1], None, ALU.mult, None, [b_u, b_cw], [b_t1])
                for tap in (1, 2):
                    self.ts("dve", t2[:, c, :], u_[:, c, tap:tap + 128], cw[:, 2 * tap + c:2 * tap + c + 1], None, ALU.mult, None, [b_u, b_cw], [b_t2])
                    self.tt("dve", t1[:, c, :], t1[:, c, :], t2[:, c, :], ALU.add, [b_t1, b_t2], [b_t1])
            self.tt("dve", ysc[:], t1[:], sc_[:, 0:2, 1:129], ALU.mult, [b_t1, bsc], [b_ysc])
            lhs = [(yT[:, 0, :], b_yT), (yT[:, 1, :], b_yT), (yT[:, 2, :], b_yT), (ysc[:, 0, :], b_ysc), (ysc[:, 1, :], b_ysc),
                   (yT[:, 3, :], b_yT), (yT[:, 4, :], b_yT), (yT[:, 5, :], b_yT)]
            for hf in range(2):
                bank = 7 - hf
                for c in range(8):
                    self.mm(P[:, bank, :], lhs[c][0], wout[:, c, hf * 512:(hf + 1) * 512], c == 0, c == 7, [lhs[c][1], b_wout], [pb[bank]], inc=(c == 7))
                self.tt("dve", x1[:, hf * 512:(hf + 1) * 512], P[:, bank, :], x[:, hf * 512:(hf + 1) * 512], ALU.add, [pb[bank], bx], [b_x1t])
            S.dma("sp", sc["x1"][s * 128:(s + 1) * 128, :], x1[:], reads=[b_x1t], writes=[sc["b_x1"][s]])

        slots = [(sg, s) for sg in self.segs() for s in self.mix_slots(sg, l)]
        prev = []
        if slots:
            front(slots[0][0], slots[0][1], 0, 0, True, False)
        for it, (sg, s) in enumerate(slots):
            b = it % 2
            b3 = it % 3
            if it + 1 < len(slots):
                front(slots[it + 1][0], slots[it + 1][1], (it + 1) % 2, (it + 1) % 3, True, False)
            S.begin_capture()
            front(sg, s, b, b3, False, True)
            F = S.end_capture()
            S.begin_capture()
            tail(sg, s, b, b3)
            T = S.end_capture()
            S.feed_merged(F, prev)
            prev = T
            self.issue_conv(2)
        S.feed(prev)

    def phase_xa(self, l):
        S, P, I, pb = self.S, self.P, self.I, self.pb
        self.phase()
        wxq, b_wxq = self.sb("wxq", [128, 8, 512], BF16)
        wxo, b_wxo = self.sb("wxo", [128, 4, D], BF16)
        pwq, b_pwq = self.sb("pwq", [128, 8, 2048], BF16)
        subk, b_subk = self.sb("subk", [128, 16, 128], BF16)
        for kc in range(8):
            S.dma("pool", wxq[:, kc, :], I["wxq"][0][l, kc * 128:(kc + 1) * 128, :], writes=[b_wxq])
            S.dma("pool", pwq[:, kc, :], I["pwq"][0][l, kc * 128:(kc + 1) * 128, :], writes=[b_pwq])
        for kc in range(4):
            S.dma("pool", wxo[:, kc, :], I["wxo"][0][l, kc * 128:(kc + 1) * 128, :], writes=[b_wxo])
        S.dma("pool", subk[:], I["subkT"][0][l], writes=[b_subk])
        kmT = {}
        vm = {}
        for sg in self.segs():
            kmT[sg] = self.sb("kmT", [128, 4, 256], BF16)
            vm[sg] = self.sb("vm", [128, 2, 4, 144], BF16)
            self.memset("dve", vm[sg][0][:], 0.0, [vm[sg][1]])
            self.memset("dve", vm[sg][0][:, :, :, 128:129], 1.0, [vm[sg][1]])
        NB = 2
        xt = [self.sb("xt", [128, D], F32) for _ in range(NB)]
        junk, b_junk = self.sb("junk", [128, D], F32)
        st, b_st = self.sb("st", [128, 4], F32)
        xn, b_xn = self.sb("xn", [128, D], BF16)
        xnT, b_xnT = self.sb("xnT", [128, 8, 128], BF16)
        sc_, b_sc = self.sb("scores", [128, 16, 128], F32)
        cand, b_cand = self.sb("cand", [128, 8, 256], F32)
        b_scg = [Buf("scg") for _ in range(16)]
        b_cag = [Buf("cag") for _ in range(8)]
        b_tsg = [Buf("tsg") for _ in range(16)]
        b_tig = [Buf("tig") for _ in range(16)]
        b_bsg = [Buf("bsg") for _ in range(8)]
        b_bpg = [Buf("bpg") for _ in range(8)]
        qx, b_qx = self.sb("qx", [128, 4, 128], BF16)
        ptx, b_ptx = self.sb("ptx", [128, 8, 128], BF16)
        rd, b_rd = self.sb("rd", [128, 4], F32)
        o_, b_o = self.sb("o", [128, 512], BF16)
        oT, b_oT = self.sb("oT", [128, 4, 128], BF16)
        x2, b_x2t = self.sb("x2", [128, D], F32)
        x3n, b_x3n = self.sb("x3n", [128, D], BF16)
        x3T, b_x3T = self.sb("x3T", [128, 8, 128], BF16)
        qT, b_qT = self.sb("qT", [128, 16, 128], BF16)
        tsv, b_tsv = self.sb("tsv", [128, 16, 16], F32)
        tiv, b_tiv = self.sb("tiv", [128, 16, 16], U32)
        tif, b_tif = self.sb("tif", [128, 16, 16], F32)
        bsv, b_bsv = self.sb("bsv", [128, 8, 16], F32)
        bpv, b_bpv = self.sb("bpv", [128, 8, 16], U32)
        k12u, b_k12u = self.sb("k12u", [128, 2, 128], U32)
        k12f, b_k12f = self.sb("k12f", [128, 2, 128], F32)
        ig, b_ig = self.sb("ig", [128, 3, 128], BF16)
        igf, b_igf = self.sb("igf", [128, 3, 128], F32)
        gsm, b_gsm = self.sb("gsm", [128, 8], F32)
        ge, b_ge = self.sb("ge", [128, 8, 16], F32)
        igT, b_igT = self.sb("igT", [128, 3, 128], BF16)

        wk = sc_[:].rearrange("p a b -> p (a b)").bitcast(BF16).rearrange("p (k n) -> p k n", k=8)
        wv = cand[:].rearrange("p a b -> p (a b)").bitcast(BF16).rearrange("p (k n) -> p k n", k=8)
        for kc in range(8):
            S.dma("pool", wk[:, kc, :], I["wxk"][0][l, kc * 128:(kc + 1) * 128, :], writes=[b_sc])
            S.dma("pool", wv[:, kc, :], I["wxv"][0][l, kc * 128:(kc + 1) * 128, :], writes=[b_cand])
        memT, b_memT = qT, b_qT
        for sg in self.segs():
            mem = I["memf" if sg == "f" else "memp"][0]
            for mt in range(2):
                x, bx = xt[mt]
                S.dma("sp", x[:], mem[mt * 128:(mt + 1) * 128, :], writes=[bx])
                self.rms(x, bx, self.gb[:, 2, :], self.b_gb, xn, b_xn, junk, b_junk, st, b_st)
                self.transpose_to(xn, b_xn, 8, 0, memT[:, mt * 8:(mt + 1) * 8, :].rearrange("p c t -> p (c t)"), b_memT)
            (km, bkm), (vmm, bvm) = kmT[sg], vm[sg]
            for hh in range(2):
                for h2 in range(2):
                    h = hh * 2 + h2
                    for mt in range(2):
                        for kc in range(8):
                            self.mm(P[:, 1, (h2 * 2 + mt) * 128:(h2 * 2 + mt + 1) * 128], wk[:, kc, h * 128:(h + 1) * 128], memT[:, mt * 8 + kc, :],
                                    kc == 0, kc == 7, [b_sc, b_memT], [pb[1]], inc=(kc == 7 and mt == 1 and h2 == 1))
                self.cp("act", km[:, hh * 2:(hh + 1) * 2, :].rearrange("p h m -> p (h m)"), P[:, 1, :], [pb[1]], [bkm])
            for mt in range(2):
                for kc in range(8):
                    self.mm(P[:, 2, :], memT[:, mt * 8 + kc, :], wv[:, kc, :], kc == 0, kc == 7, [b_cand, b_memT], [pb[2]], inc=(kc == 7))
                self.cp("dve", vmm[:, mt, :, 0:128], P[:, 2, :].rearrange("p (h d) -> p h d", d=128), [pb[2]], [bvm])

        TT = [dict(sc=sc_, b_sc=b_sc, cand=cand, b_cand=b_cand, b_scg=b_scg, b_cag=b_cag, b_tsg=b_tsg, b_tig=b_tig, b_bsg=b_bsg, b_bpg=b_bpg,
                   tsv=tsv, tiv=tiv, tif=tif, b_tif=b_tif, bsv=bsv, bpv=bpv, k12u=k12u, b_k12u=b_k12u, k12f=k12f, b_k12f=b_k12f,
                   ig=ig, b_ig=b_ig, igf=igf, b_igf=b_igf, gsm=gsm, b_gsm=b_gsm, ge=ge, b_ge=b_ge, igT=igT, b_igT=b_igT)]
        d = {}
        d["sc"], d["b_sc"] = self.sb("scores", [128, 16, 128], F32)
        d["cand"], d["b_cand"] = self.sb("cand", [128, 8, 256], F32)
        for nm, n in (("b_scg", 16), ("b_cag", 8), ("b_tsg", 16), ("b_tig", 16), ("b_bsg", 8), ("b_bpg", 8)):
            d[nm] = [Buf(nm) for _ in range(n)]
        d["tsv"], _ = self.sb("tsv", [128, 16, 16], F32)
        d["tiv"], _ = self.sb("tiv", [128, 16, 16], U32)
        d["tif"], d["b_tif"] = self.sb("tif", [128, 16, 16], F32)
        d["bsv"], _ = self.sb("bsv", [128, 8, 16], F32)
        d["bpv"], _ = self.sb("bpv", [128, 8, 16], U32)
        d["k12u"], d["b_k12u"] = self.sb("k12u", [128, 2, 128], U32)
        d["k12f"], d["b_k12f"] = self.sb("k12f", [128, 2, 128], F32)
        d["ig"], d["b_ig"] = self.sb("ig", [128, 3, 128], BF16)
        d["igf"], d["b_igf"] = self.sb("igf", [128, 3, 128], F32)
        d["gsm"], d["b_gsm"] = self.sb("gsm", [128, 8], F32)
        d["ge"], d["b_ge"] = self.sb("ge", [128, 8, 16], F32)
        d["igT"], d["b_igT"] = self.sb("igT", [128, 3, 128], BF16)
        TT.append(d)

        def front(sg, s, b):
            sc = self.SC[sg]
            (km, bkm), (vmm, bvm) = kmT[sg], vm[sg]
            T = TT[b]
            x, bx = xt[b]
            S.dma("sp", x[:], sc["x1"][s * 128:(s + 1) * 128, :], reads=[sc["b_x1"][s]], writes=[bx])
            self.rms(x, bx, self.gb[:, 1, :], self.b_gb, xn, b_xn, junk, b_junk, st, b_st)
            self.transpose_to(xn, b_xn, 8, 0, xnT[:].rearrange("p c t -> p (c t)"), b_xnT)
            for h in range(4):
                for kc in range(8):
                    self.mm(P[:, 1, h * 128:(h + 1) * 128], wxq[:, kc, h * 128:(h + 1) * 128], xnT[:, kc, :], kc == 0, kc == 7,
                            [b_wxq, b_xnT], [pb[1]], inc=(kc == 7 and h == 3))
            self.cp("act", qx[:].rearrange("p h t -> p (h t)"), P[:, 1, :], [pb[1]], [b_qx])
            for hh in range(2):
                bank = 2 + hh
                for h2 in range(2):
                    h = hh * 2 + h2
                    for mt in range(2):
                        j = h2 * 2 + mt
                        self.mm(P[:, bank, j * 128:(j + 1) * 128], km[:, h, mt * 128:(mt + 1) * 128], qx[:, h, :], True, True,
                                [bkm, b_qx], [pb[bank]], inc=(j == 3))
                self.act(ptx[:, hh * 4:(hh + 1) * 4, :].rearrange("p a t -> p (a t)"), P[:, bank, :], AF.Exp, [pb[bank]], [b_ptx], scale=float(128 ** -0.5))
            for hh in range(2):
                bank = 4 if hh == 0 else 6
                for h2 in range(2):
                    h = hh * 2 + h2
                    for mt in range(2):
                        self.mm(P[:, bank, h2 * 144:h2 * 144 + 129], ptx[:, h * 2 + mt, :], vmm[:, mt, h, 0:129], mt == 0, mt == 1,
                                [b_ptx, bvm], [pb[bank]], inc=(mt == 1 and h2 == 1))
                pv = P[:, bank, 0:288].rearrange("p (h d) -> p h d", d=144)
                S.op("dve", lambda e, pv=pv, hh=hh: e.reciprocal(out=rd[:, hh * 2:(hh + 1) * 2], in_=pv[:, :, 128]), reads=[pb[bank]], writes=[b_rd])
                self.tt("dve", o_[:, hh * 256:(hh + 1) * 256].rearrange("p (h d) -> p h d", d=128), pv[:, :, 0:128],
                        rd[:, hh * 2:(hh + 1) * 2].unsqueeze(2).to_broadcast([128, 2, 128]), ALU.mult, [pb[bank], b_rd], [b_o])
            self.transpose_to(o_, b_o, 4, 0, oT[:].rearrange("p c t -> p (c t)"), b_oT)
            for hf in range(2):
                for c in range(4):
                    self.mm(P[:, 6 + hf, :], oT[:, c, :], wxo[:, c, hf * 512:(hf + 1) * 512], c == 0, c == 3, [b_oT, b_wxo], [pb[6 + hf]], inc=(c == 3))
            self.tt("dve", x2[:].rearrange("p (a d) -> p a d", a=2), P[:, 6:8, :], x[:].rearrange("p (a d) -> p a d", a=2), ALU.add, [pb[6], pb[7], bx], [b_x2t])
            S.dma("sp", sc["x2"][s * 128:(s + 1) * 128, :], x2[:], reads=[b_x2t], writes=[sc["b_x2"][s]])
            self.rms(x2, b_x2t, self.gb[:, 3, :], self.b_gb, x3n, b_x3n, junk, b_junk, st, b_st)
            self.transpose_to(x3n, b_x3n, 8, 0, x3T[:].rearrange("p c t -> p (c t)"), b_x3T)
            S.dma("sp", sc["xnT"][s], x3T[:], reads=[b_x3T], writes=[sc["b_xnT"][s]])
            for g in range(4):
                bank = 1 + (g % 2)
                for j in range(4):
                    c = g * 4 + j
                    for kc in range(8):
                        self.mm(P[:, bank, j * 128:(j + 1) * 128], pwq[:, kc, c * 128:(c + 1) * 128], x3T[:, kc, :], kc == 0, kc == 7,
                                [b_pwq, b_x3T], [pb[bank]], inc=(kc == 7 and j == 3))
                self.cp("act", qT[:, g * 4:(g + 1) * 4, :].rearrange("p c t -> p (c t)"), P[:, bank, :], [pb[bank]], [b_qT])
            for g in range(4):
                bank = 3 + (g % 2)
                for j in range(4):
                    c = g * 4 + j
                    self.mm(P[:, bank, j * 128:(j + 1) * 128], qT[:, c, :], subk[:, c, :], True, True, [b_qT, b_subk], [pb[bank]], inc=(j == 3))
                self.cp("act", T["sc"][:, g * 4:(g + 1) * 4, :].rearrange("p c n -> p (c n)"), P[:, bank, :], [pb[bank]], [T["b_sc"]] + T["b_scg"][g * 4:(g + 1) * 4])

        def tail(sg, s, b):
            sc = self.SC[sg]
            T = TT[b]
            sc_t, cand_t, tsv_t, tiv_t, tif_t, bsv_t, bpv_t = T["sc"], T["cand"], T["tsv"], T["tiv"], T["tif"], T["bsv"], T["bpv"]
            k12u_t, k12f_t, ig_t, igf_t, gsm_t, ge_t, igT_t = T["k12u"], T["k12f"], T["ig"], T["igf"], T["gsm"], T["ge"], T["igT"]
            self.top16(sc_t, T["b_scg"], 16, tsv_t, T["b_tsg"], tiv_t, T["b_tig"])
            self.cp("dve", tif_t[:], tiv_t[:], T["b_tig"], [T["b_tif"]])
            ts4 = tsv_t[:].rearrange("p (h a) k -> p h a k", a=2)
            self.tt("pool", cand_t[:].rearrange("p h (a b) -> p h a b", b=16),
                    ts4[:, :, 0, :].unsqueeze(3).to_broadcast([128, 8, 16, 16]),
                    ts4[:, :, 1, :].unsqueeze(2).to_broadcast([128, 8, 16, 16]), ALU.add, T["b_tsg"], [T["b_cand"]] + T["b_cag"])
            self.top16(cand_t, T["b_cag"], 8, bsv_t, T["b_bsg"], bpv_t, T["b_bpg"])
            self.ts("dve", k12u_t[:, 0, :], bpv_t[:].rearrange("p h k -> p (h k)"), 4, None, ALU.arith_shift_right, None, T["b_bpg"], [T["b_k12u"]])
            self.ts("dve", k12u_t[:, 1, :], bpv_t[:].rearrange("p h k -> p (h k)"), 15, None, ALU.bitwise_and, None, T["b_bpg"], [T["b_k12u"]])
            self.cp("dve", k12f_t[:], k12u_t[:], [T["b_k12u"]], [T["b_k12f"]])
            tif4 = tif_t[:].rearrange("p (h a) k -> p h a k", a=2)
            oh = sc_t[:].rearrange("p a b -> p (a b)").rearrange("p (h k j) -> p h k j", h=8, k=16)
            for a in range(2):
                self.tt("dve", oh, k12f_t[:, a, :].rearrange("p (h k) -> p h k", h=8).unsqueeze(3).to_broadcast([128, 8, 16, 16]),
                        self.iota16[:].unsqueeze(1).unsqueeze(1).to_broadcast([128, 8, 16, 16]), ALU.is_equal, [T["b_k12f"], self.b_iota16], [T["b_sc"]] + T["b_scg"])
                self.tt("pool", oh, oh, tif4[:, :, a, :].unsqueeze(2).to_broadcast([128, 8, 16, 16]), ALU.mult, [T["b_sc"], T["b_tif"]], [T["b_sc"]])
                S.op("dve", lambda e, a=a, oh=oh, igf_t=igf_t: e.tensor_reduce(out=igf_t[:, a, :], in_=oh.rearrange("p h k j -> p (h k) j"), axis=AX.X, op=ALU.add), reads=[T["b_sc"]], writes=[T["b_igf"]])
            self.tt("dve", ge_t[:], bsv_t[:], bsv_t[:, :, 0:1].to_broadcast([128, 8, 16]), ALU.subtract, T["b_bsg"], [T["b_ge"]])
            self.act(ge_t[:], ge_t[:], AF.Exp, [T["b_ge"]], [T["b_ge"]])
            S.op("dve", lambda e, gsm_t=gsm_t, ge_t=ge_t: e.tensor_reduce(out=gsm_t[:], in_=ge_t[:], axis=AX.X, op=ALU.add), reads=[T["b_ge"]], writes=[T["b_gsm"]])
            S.op("dve", lambda e, gsm_t=gsm_t: e.reciprocal(out=gsm_t[:], in_=gsm_t[:]), reads=[T["b_gsm"]], writes=[T["b_gsm"]])
            self.tt("dve", igf_t[:, 2, :].rearrange("p (h k) -> p h k", h=8), ge_t[:], gsm_t[:].unsqueeze(2).to_broadcast([128, 8, 16]), ALU.mult, [T["b_ge"], T["b_gsm"]], [T["b_igf"]])
            self.cp("dve", ig_t[:], igf_t[:], [T["b_igf"]], [T["b_ig"]])
            self.transpose_to(ig_t[:].rearrange("p a t -> p (a t)"), T["b_ig"], 3, 5, igT_t[:].rearrange("p c t -> p (c t)"), T["b_igT"])
            S.dma("sp", sc["ixT"][s], igT_t[:], reads=[T["b_igT"]], writes=[sc["b_ixT"][s]])

        slots = [(sg, s) for sg in self.segs() for s in self.mix_slots(sg, l)]
        prev = []
        for it, (sg, s) in enumerate(slots):
            b = it % 2
            S.begin_capture()
            front(sg, s, b)
            F = S.end_capture()
            S.begin_capture()
            tail(sg, s, b)
            Tl = S.end_capture()
            S.feed_merged(F, prev)
            prev = Tl
            self.issue_conv(2)
        S.feed(prev)

    def top16(self, src, bsrc, ngrp, vals, bvals, idx, bidx):
        S = self.S
        for g in range(ngrp):
            sl = src[:, g, :]
            S.op("dve", lambda e, g=g, sl=sl: e.max(out=vals[:, g, 0:8], in_=sl), reads=[bsrc[g]], writes=[bvals[g]])
        for g in range(ngrp):
            sl = src[:, g, :]
            S.op("dve", lambda e, g=g, sl=sl: e.max_index(out=idx[:, g, 0:8], in_max=vals[:, g, 0:8], in_values=sl), reads=[bsrc[g], bvals[g]], writes=[bidx[g]])
        for g in range(ngrp):
            sl = src[:, g, :]
            S.op("dve", lambda e, g=g, sl=sl: e.match_replace(out=sl, in_to_replace=vals[:, g, 0:8], in_values=sl, imm_value=NEG), reads=[bsrc[g], bvals[g]], writes=[bsrc[g]])
        for g in range(ngrp):
            sl = src[:, g, :]
            S.op("dve", lambda e, g=g, sl=sl: e.max(out=vals[:, g, 8:16], in_=sl), reads=[bsrc[g]], writes=[bvals[g]])
        for g in range(ngrp):
            sl = src[:, g, :]
            S.op("dve", lambda e, g=g, sl=sl: e.max_index(out=idx[:, g, 8:16], in_max=vals[:, g, 8:16], in_values=sl), reads=[bsrc[g], bvals[g]], writes=[bidx[g]])

    def phase_peer(self, l):
        S, P, I, pb = self.S, self.P, self.I, self.pb
        cfg = self.cfg
        self.issue_conv(None, upto_layer=l)
        self.phase()
        last = (l == DEPTH - 1)
        W, b_W = self.sb("W", [128, 3 * 128, 128], BF16)
        Rb = [self.sb("R", [128, 128, 32], BF16) for _ in range(2)]
        Cb = [self.sb("C", [128, 128, 32], BF16) for _ in range(2)]
        NU = 4
        ub = [self.sb("ub", [128, 8, 128], BF16) for _ in range(NU)]
        vb = [self.sb("vb", [128, D], BF16) for _ in range(NU)]
        XT, b_XT = self.sb("XT", [128, 3, 8, 128], BF16)
        ix, b_ix = self.sb("ix", [128, 3, 3, 128], BF16)
        g0b = [self.sb("g0", [128, 384], BF16) for _ in range(4)]
        g1b = [self.sb("g1", [128, 384], BF16) for _ in range(4)]
        x2t, b_x2t = self.sb("x2t", [128, D], F32)
        x3t, b_x3t = self.sb("x3t", [128, D], F32)
        junk, b_junk = self.sb("junk", [128, D], F32)
        st, b_st = self.sb("st", [128, 4], F32)
        yo, b_yo = self.sb("yo", [128, D], F32)
        slots = [(sg, s) for sg in self.segs() for s in self.mix_slots(sg, l)]
        groups = [slots[i:i + 3] for i in range(0, len(slots), 3)]
        iobf, b_iobf = self.sb("iobf", [128, 128], BF16)
        self.cp("dve", iobf[:], self.iota128[:], [self.b_iota128], [b_iobf])
        iorep, b_iorep = self.sb("iorep", [128, 128, 32], BF16)
        self.cp("dve", iorep[:], iobf[:].unsqueeze(2).to_broadcast([128, 128, 32]), [b_iobf], [b_iorep])
        iob = iorep[:]
        qi = 0
        cc = 0
        for grp in groups:
            nt = len(grp)
            for ti, (sg, s) in enumerate(grp):
                sc = self.SC[sg]
                S.dma("sp", XT[:, ti], sc["xnT"][s], reads=[sc["b_xnT"][s]], writes=[b_XT])
                S.dma("sp", ix[:, ti], sc["ixT"][s], reads=[sc["b_ixT"][s]], writes=[b_ix])
            for ti in range(nt):
                for hf in range(4):
                    t0 = hf * 32
                    (R, b_R), (C, b_C) = Rb[qi % 2], Cb[qi % 2]
                    qi += 1
                    i1b = ix[:, ti, 0, t0:t0 + 32].unsqueeze(1).to_broadcast([128, 128, 32])
                    i2b = ix[:, ti, 1, t0:t0 + 32].unsqueeze(1).to_broadcast([128, 128, 32])
                    gbb = ix[:, ti, 2, t0:t0 + 32].unsqueeze(1).to_broadcast([128, 128, 32])
                    self.tt("dve", R[:], iob, i1b, ALU.is_equal, [b_iorep, b_ix], [b_R])
                    self.tt("pool" if qi % 3 != 0 else "dve", R[:], R[:], gbb, ALU.mult, [b_R, b_ix], [b_R])
                    self.tt("dve", C[:], iob, i2b, ALU.is_equal, [b_iorep, b_ix], [b_C])
                    for q4 in range(8):
                        bank = 6 + (q4 % 2)
                        for j in range(4):
                            t = q4 * 4 + j
                            self.mm(P[:, bank, j * 128:(j + 1) * 128], R[:, :, t], C[:, :, t], True, True, [b_R, b_C], [pb[bank]], inc=(j == 3))
                        tok0 = ti * 128 + t0 + q4 * 4
                        self.cp("act", W[:, tok0:tok0 + 4, :].rearrange("p t i -> p (t i)"), P[:, bank, :], [pb[bank]], [b_W])
            N = nt * 128
            XTf = XT[:].rearrange("p t k n -> p t (k n)")

            def load_uv(c, cc):
                u, bu = ub[cc % NU]
                v, bv = vb[cc % NU]
                S.dma("sp", u[:].rearrange("p k e -> p (k e)"), self.Ubf[l, c], reads=[self.b_Ubf[l][c]], writes=[bu])
                S.dma("sp", v[:], self.Vbf[l, c], reads=[self.b_Vbf[l][c]], writes=[bv])

            def issue_A(c, cc):
                u, bu = ub[cc % NU]
                bank = 6 + (cc % 2)
                for ti in range(nt):
                    for kc in range(8):
                        self.mm(P[:, bank, ti * 128:(ti + 1) * 128], u[:, kc, :], XT[:, ti, kc, :], kc == 0, kc == 7, [bu, b_XT], [pb[bank]],
                                inc=(kc == 7 and ti == nt - 1))

            PF = NU - 1
            for c in range(PF):
                load_uv(c, cc + c)

            def stage_A(c, cc):
                issue_A(c, cc)
                bank = 6 + (cc % 2)
                (g0, bg0), (g1, bg1) = g0b[cc % 4], g1b[cc % 4]
                self.act(g0[:, 0:N], P[:, bank, 0:N], AF.Gelu, [pb[bank]], [bg0])
                self.tt("dve", g1[:, 0:N], g0[:, 0:N], W[:, 0:N, c], ALU.mult, [bg0, b_W], [bg1])

            stage_A(0, cc)
            stage_A(1, cc + 1)
            for c in range(128):
                if c + PF < 128:
                    load_uv(c + PF, cc + PF)
                if c + 2 < 128:
                    stage_A(c + 2, cc + 2)
                v, bv = vb[cc % NU]
                g1, bg1 = g1b[cc % 4]
                for ti in range(nt):
                    for hf in range(2):
                        self.mm(P[:, 2 * ti + hf, :], g1[:, ti * 128:(ti + 1) * 128], v[:, hf * 512:(hf + 1) * 512], c == 0, c == 127,
                                [bg1, bv], [pb[2 * ti + hf]], inc=(hf == 1 and ti == nt - 1))
                cc += 1
            for ti, (sg, s) in enumerate(grp):
                sc = self.SC[sg]
                S.dma("sp", x2t[:], sc["x2"][s * 128:(s + 1) * 128, :], reads=[sc["b_x2"][s]], writes=[b_x2t])
                self.tt("dve", x3t[:].rearrange("p (a d) -> p a d", a=2), P[:, 2 * ti:2 * ti + 2, :], x2t[:].rearrange("p (a d) -> p a d", a=2), ALU.add,
                        [pb[2 * ti], pb[2 * ti + 1], b_x2t], [b_x3t])
                if not last:
                    S.dma("sp", sc["x3"][s * 128:(s + 1) * 128, :], x3t[:], reads=[b_x3t], writes=[sc["b_x3"][s]])
                else:
                    self.act(junk[:], x3t[:], AF.Square, [b_x3t], [b_junk, b_st], accum_out=st[:, 0:1])
                    self.ts("dve", st[:, 1:2], st[:, 0:1], 1.0 / D, 1e-6, ALU.mult, ALU.add, [b_st], [b_st])
                    self.act(st[:, 2:3], st[:, 1:2], AF.Ln, [b_st], [b_st])
                    self.act(st[:, 2:3], st[:, 2:3], AF.Exp, [b_st], [b_st], scale=-0.5)
                    self.stt("dve", yo[:], x3t[:], st[:, 2:3], self.gfin[:], ALU.mult, ALU.mult, [b_x3t, b_st, self.b_gfin], [b_yo])
                    if sg == "f":
                        dst, bd = self.O["yf"][0][s * 128:(s + 1) * 128, :], self.O["yf"][1][s]
                    else:
                        so = s - 2 * cfg.H
                        dst, bd = self.O["yp"][0][so * 128:(so + 1) * 128, :], self.O["yp"][1][so]
                    S.dma("sp", dst, yo[:], reads=[b_yo], writes=[bd])


def prep_shared(inp, cfg):
    f = lambda a: np.ascontiguousarray(np.asarray(a, dtype=np.float32))
    sh = {}
    w_in = f(inp["w_in"])
    swq = np.concatenate([np.arange(1920 + h * 64, 1920 + (h + 1) * 64) for h in (0, 3, 1, 4, 2, 5)])
    cols = np.concatenate([np.arange(0, 768), np.arange(1152, 1920), swq, np.arange(2304, 2432), np.arange(768, 1152), np.arange(2432, 2560)])
    sh["w_in"] = np.ascontiguousarray(w_in[:, :, cols])
    sh["w_out"] = f(inp["w_out"])
    for a, b in (("wxq", "w_xq"), ("wxk", "w_xk"), ("wxv", "w_xv"), ("wxo", "w_xo"), ("pwq", "peer_wq")):
        sh[a] = f(inp[b])
    sk = f(inp["peer_subkeys"])
    sh["subkT"] = np.ascontiguousarray(np.transpose(sk, (0, 4, 1, 2, 3)).reshape(DEPTH, 128, 16, 128))
    u = f(inp["peer_u"])
    v = f(inp["peer_v"])
    sh["u_arr"] = np.ascontiguousarray(u.reshape(DEPTH, 128, 128, 8, 128).transpose(0, 2, 4, 3, 1).reshape(DEPTH, 128, 128, D))
    sh["v_arr"] = np.ascontiguousarray(v.reshape(DEPTH, 128, 128, D).transpose(0, 2, 1, 3))
    g = np.stack([f(inp["norm_mix_g"]), f(inp["norm_xa_g"]), f(inp["norm_mem_g"]), f(inp["norm_ffn_g"])], axis=1)
    sh["gains"] = np.ascontiguousarray(np.broadcast_to(g[:, :, None, :], (DEPTH, 4, 128, D)))
    sh["gfinal"] = np.ascontiguousarray(np.broadcast_to(f(inp["final_g"])[None, :], (128, D)))
    cw = f(inp["conv_w"])
    sh["convw"] = np.ascontiguousarray(cw.reshape(DEPTH, 3, 2, 128).transpose(0, 3, 1, 2).reshape(DEPTH, 128, 6))
    sh["sink"] = np.ascontiguousarray(np.broadcast_to(f(inp["swa_sink"])[:, None, :], (DEPTH, 128, 6)))
    rpb = f(inp["na_rpb"])
    bias_off = f(inp["t5_bias"])[t5_bucket(np.arange(-128, 129))]
    sh["_rpb"] = rpb
    sh["_bias_off"] = bias_off
    nt = cfg.NT_F
    ji = nt // 2
    sh["na_int"] = np.stack([na_bias_block(rpb[l], nt, ji, list(range(ji - 2, ji + 3))).reshape(128, -1) for l in range(DEPTH)])
    sh["swa_int"] = np.stack([swa_bias_block(bias_off, nt, ji, [ji - 1, ji, ji + 1]).reshape(128, -1)] * DEPTH)
    spec = cfg.f_special()
    na_fs = np.zeros((DEPTH, len(spec), 128, 6 * 7 * 128), np.float32)
    for l in range(DEPTH):
        for i, j in enumerate(spec):
            k0, k1 = na_key_range(j, nt)
            blk = na_bias_block(rpb[l], nt, j, list(range(k0, k1 + 1))).reshape(128, -1)
            na_fs[l, i, :, :blk.shape[1]] = blk
    sh["na_fs"] = na_fs
    swa_fs = np.zeros((DEPTH, 2, 128, 6 * 3 * 128), np.float32)
    for i, j in enumerate(cfg.f_swa_special()):
        keys = list(range(max(j - 1, 0), min(j + 1, nt - 1) + 1))
        blk = swa_bias_block(bias_off, nt, j, keys).reshape(128, -1)
        swa_fs[:, i, :, :blk.shape[1]] = blk
    sh["swa_fs"] = swa_fs
    return sh


def prep_core(sh, cfg, x_full, mem_full, x_pseq=None, mem_p=None, q=0):
    m = {k: v for k, v in sh.items() if not k.startswith("_")}
    m["xf"] = np.ascontiguousarray(x_full, dtype=np.float32)
    m["memf"] = np.ascontiguousarray(mem_full, dtype=np.float32)
    if cfg.S_P:
        S_P, H = cfg.S_P, cfg.H
        g0 = q * cfg.NP_OUT - 2 * H
        xp = np.zeros((S_P * 128, D), np.float32)
        valid = np.zeros((128, S_P), np.float32)
        na_p = np.zeros((DEPTH, S_P, 128, 6 * 7 * 128), np.float32)
        swa_p = np.zeros((DEPTH, S_P, 128, 6 * 3 * 128), np.float32)
        for s in range(S_P):
            g = g0 + s
            if 0 <= g < cfg.NT_P:
                xp[s * 128:(s + 1) * 128] = x_pseq[g * 128:(g + 1) * 128]
                valid[:, s] = 1.0
            sw = swa_bias_block(sh["_bias_off"], cfg.NT_P, g, [g - 1, g, g + 1]).reshape(128, -1)
            for l in range(DEPTH):
                na_p[l, s] = na_bias_block(sh["_rpb"][l], cfg.NT_P, g, list(range(g - 3, g + 4))).reshape(128, -1)
                swa_p[l, s] = sw
        m["xp"] = xp
        m["validp"] = valid
        m["memp"] = np.ascontiguousarray(mem_p, dtype=np.float32)
        m["na_p"] = na_p
        m["swa_p"] = swa_p
    return m


_NC_CACHE = {}


def get_nc(cfg, dbg=False, stop=None, conv=True):
    key = (cfg.NT_F, cfg.NT_P, cfg.NP_OUT, cfg.H, dbg, stop, conv)
    if key not in _NC_CACHE:
        k = K(cfg, dbg, stop, conv)
        _NC_CACHE[key] = k.build()
    return _NC_CACHE[key]


def kernel(**inp):
    cfg = Cfg(64, 64, 16, 3)
    sh = prep_shared(inp, cfg)
    xs = np.asarray(inp["x_sample"], dtype=np.float32)
    xp = np.asarray(inp["x_prompt"], dtype=np.float32)
    ms = np.asarray(inp["mem_sample"], dtype=np.float32)
    mp = np.asarray(inp["mem_prompt"], dtype=np.float32)
    in_maps = []
    for c in range(8):
        p, q = c // 4, c % 4
        in_maps.append(prep_core(sh, cfg, xs[c], ms[c], xp[p], mp[p], q))
    nc = get_nc(cfg)
    res = run_bass_kernel_spmd(nc, in_maps, core_ids=list(range(8)))
    y_sample = np.stack([np.asarray(res.results[c]["yf"], dtype=np.float32).reshape(8192, D) for c in range(8)])
    y_prompt = np.zeros((2, 8192, D), np.float32)
    for c in range(8):
        p, q = c // 4, c % 4
        y_prompt[p, q * 2048:(q + 1) * 2048] = np.asarray(res.results[c]["yp"], dtype=np.float32)
    return (y_prompt, y_sample)
```
